# Optimizing a Trainium2 kernel written in Bass

```python
import math
import jax, jax.numpy as jnp
from jax import lax
import numpy as np

D_MODEL = 1024
BATCH = 4
SEQ = 8192
DEPTH = 4
DEC_BATCH = 2
DEC_SEQ = 8192
PAST_LEN = 128

MLA_HEADS = 8
MLA_Q_LORA = 256
MLA_KV_LORA = 128
MLA_NOPE = 64
MLA_ROPE = 32
MLA_V = 64
MLA_QK = MLA_NOPE + MLA_ROPE
MLA_THETA = 10000.0
DIFF_HEADS = 4
DIFF_DK = 64
DIFF_DV = 2 * DIFF_DK
DIFF_ROPE = DIFF_DK // 4
ROPE_THETA = 500000.0
MIX_WIDTH = MLA_HEADS * MLA_V + DIFF_HEADS * DIFF_DV
IN_SIZES = (MLA_Q_LORA, MLA_KV_LORA, MLA_ROPE,
            DIFF_HEADS * 2 * DIFF_DK, DIFF_HEADS * 2 * DIFF_DK, DIFF_HEADS * DIFF_DV)
IN_DIM = sum(IN_SIZES)
IN_SPLITS = tuple(int(v) for v in np.cumsum(IN_SIZES)[:-1])
D_FF = 2816
CONV_W = 3
Q_BLOCK = 128
EPS = 1e-6

kernel_name = "hybrid_mla_diffattn_convffn_encoder"


def _rms_norm(x, g):
    xf = x.astype(jnp.float32)
    y = xf * lax.rsqrt(jnp.mean(xf * xf, axis=-1, keepdims=True) + EPS)
    return (y * g.astype(jnp.float32)).astype(x.dtype)


def _rope(x, theta):
    s, d = x.shape[1], x.shape[-1]
    half = d // 2
    freqs = theta ** (-jnp.arange(half, dtype=jnp.float32) * 2.0 / d)
    ang = jnp.arange(s, dtype=jnp.float32)[:, None] * freqs[None, :]
    shape = (1, s) + (1,) * (x.ndim - 3) + (half,)
    cos = jnp.cos(ang).reshape(shape)
    sin = jnp.sin(ang).reshape(shape)
    xf = x.astype(jnp.float32)
    x1, x2 = xf[..., :half], xf[..., half:]
    return jnp.concatenate([x1 * cos - x2 * sin, x2 * cos + x1 * sin], axis=-1).astype(x.dtype)


def _over_query_blocks(fn, q):
    b, s = q.shape[:2]
    nb = s // Q_BLOCK
    qb = jnp.moveaxis(q.reshape((b, nb, Q_BLOCK) + q.shape[2:]), 1, 0)
    ob = jnp.moveaxis(lax.map(fn, qb), 0, 1)
    return ob.reshape((b, s) + ob.shape[3:])


def _mla_attention(q, k, v):
    scale = MLA_QK ** -0.5

    def blk(qi):
        sc = jnp.einsum('bqhd,bkhd->bhqk', qi, k).astype(jnp.float32) * scale
        p = jax.nn.softmax(sc, axis=-1).astype(v.dtype)
        return jnp.einsum('bhqk,bkhd->bqhd', p, v)

    return _over_query_blocks(blk, q)


def _diff_attention(q, k, v, lam):
    scale = DIFF_DK ** -0.5

    def blk(qi):
        sc = jnp.einsum('bqhmd,bkhmd->bhmqk', qi, k).astype(jnp.float32) * scale
        p = jax.nn.softmax(sc, axis=-1)
        a = (p[:, :, 0] - lam * p[:, :, 1]).astype(v.dtype)
        return jnp.einsum('bhqk,bkhd->bqhd', a, v)

    return _over_query_blocks(blk, q)


def _dwconv3(h, w, b):
    hp = jnp.pad(h, ((0, 0), (1, 1), (0, 0)))
    return hp[:, :-2] * w[0] + hp[:, 1:-1] * w[1] + hp[:, 2:] * w[2] + b


def _layer(x, l, p):
    b, s, _ = x.shape
    h = _rms_norm(x, p['ln1_g'])
    proj = h @ p['w_in']
    c_q, c_kv, k_pe, dq, dk, dv = jnp.split(proj, IN_SPLITS, axis=-1)

    q = (_rms_norm(c_q, p['mla_q_norm_g']) @ p['w_q_up']).reshape(b, s, MLA_HEADS, MLA_QK)
    kv = (_rms_norm(c_kv, p['mla_kv_norm_g']) @ p['w_kv_up']).reshape(b, s, MLA_HEADS, MLA_NOPE + MLA_V)
    k_nope, v_m = kv[..., :MLA_NOPE], kv[..., MLA_NOPE:]
    k_m = jnp.concatenate(
        [k_nope, jnp.broadcast_to(k_pe[:, :, None, :], (b, s, MLA_HEADS, MLA_ROPE))], axis=-1)
    q = _rms_norm(q, p['mla_qn_g'])
    k_m = _rms_norm(k_m, p['mla_kn_g'])
    q = jnp.concatenate([q[..., :MLA_NOPE], _rope(q[..., MLA_NOPE:], MLA_THETA)], axis=-1)
    k_m = jnp.concatenate([k_m[..., :MLA_NOPE], _rope(k_m[..., MLA_NOPE:], MLA_THETA)], axis=-1)
    o_mla = _mla_attention(q, k_m, v_m).reshape(b, s, MLA_HEADS * MLA_V)

    qd = _rms_norm(dq.reshape(b, s, DIFF_HEADS, 2, DIFF_DK), p['diff_qn_g'])
    kd = _rms_norm(dk.reshape(b, s, DIFF_HEADS, 2, DIFF_DK), p['diff_kn_g'])
    vd = dv.reshape(b, s, DIFF_HEADS, DIFF_DV)
    qd = jnp.concatenate([_rope(qd[..., :DIFF_ROPE], ROPE_THETA), qd[..., DIFF_ROPE:]], axis=-1)
    kd = jnp.concatenate([_rope(kd[..., :DIFF_ROPE], ROPE_THETA), kd[..., DIFF_ROPE:]], axis=-1)
    lam_init = 0.8 - 0.6 * math.exp(-0.3 * l)
    f32 = jnp.float32
    lam = (jnp.exp(jnp.sum(p['lambda_q1'].astype(f32) * p['lambda_k1'].astype(f32)))
           - jnp.exp(jnp.sum(p['lambda_q2'].astype(f32) * p['lambda_k2'].astype(f32)))
           + lam_init)
    o_d = _diff_attention(qd, kd, vd, lam)
    o_d = (_rms_norm(o_d, p['diff_subln_g']) * (1.0 - lam_init)).reshape(b, s, DIFF_HEADS * DIFF_DV)

    x = x + jnp.concatenate([o_mla, o_d], axis=-1) @ p['w_out']

    h = _rms_norm(x, p['ln2_g'])
    g = _dwconv3(h @ p['w_gate'], p['conv_w'], p['conv_b'])
    x = x + (jax.nn.silu(g) * (h @ p['w_up'])) @ p['w_down']
    return x


def _trunk(x, params):
    for l in range(DEPTH):
        x = _layer(x, l, {name: arr[l] for name, arr in params.items()})
    return x


def setup_inputs(seed: int = 0) -> dict:
    key = jax.random.key(seed)
    ks = jax.random.split(key, 28)

    def nrm(k, shape, scale):
        return jax.random.normal(k, shape, jnp.float32) * scale

    def gain(k, n):
        return 1.0 + nrm(k, (DEPTH, n), 0.02)

    return {
        'x_prompt': nrm(ks[0], (BATCH, SEQ, D_MODEL), 1.0),
        'x_sample': nrm(ks[1], (DEC_BATCH, DEC_SEQ, D_MODEL), 1.0),
        'ln1_g': gain(ks[2], D_MODEL),
        'w_in': nrm(ks[3], (DEPTH, D_MODEL, IN_DIM), D_MODEL ** -0.5),
        'mla_q_norm_g': gain(ks[4], MLA_Q_LORA),
        'w_q_up': nrm(ks[5], (DEPTH, MLA_Q_LORA, MLA_HEADS * MLA_QK), MLA_Q_LORA ** -0.5),
        'mla_kv_norm_g': gain(ks[6], MLA_KV_LORA),
        'w_kv_up': nrm(ks[7], (DEPTH, MLA_KV_LORA, MLA_HEADS * (MLA_NOPE + MLA_V)), MLA_KV_LORA ** -0.5),
        'mla_qn_g': gain(ks[8], MLA_QK),
        'mla_kn_g': gain(ks[9], MLA_QK),
        'diff_qn_g': gain(ks[10], DIFF_DK),
        'diff_kn_g': gain(ks[11], DIFF_DK),
        'lambda_q1': nrm(ks[12], (DEPTH, DIFF_DK), 0.1),
        'lambda_k1': nrm(ks[13], (DEPTH, DIFF_DK), 0.1),
        'lambda_q2': nrm(ks[14], (DEPTH, DIFF_DK), 0.1),
        'lambda_k2': nrm(ks[15], (DEPTH, DIFF_DK), 0.1),
        'diff_subln_g': gain(ks[16], DIFF_DV),
        'w_out': nrm(ks[17], (DEPTH, MIX_WIDTH, D_MODEL), 0.5 * MIX_WIDTH ** -0.5),
        'ln2_g': gain(ks[18], D_MODEL),
        'w_gate': nrm(ks[19], (DEPTH, D_MODEL, D_FF), D_MODEL ** -0.5),
        'conv_w': nrm(ks[20], (DEPTH, CONV_W, D_FF), CONV_W ** -0.5),
        'conv_b': nrm(ks[21], (DEPTH, D_FF), 0.01),
        'w_up': nrm(ks[22], (DEPTH, D_MODEL, D_FF), D_MODEL ** -0.5),
        'w_down': nrm(ks[23], (DEPTH, D_FF, D_MODEL), 0.5 * D_FF ** -0.5),
    }


def reference(x_prompt, x_sample, ln1_g, w_in, mla_q_norm_g, w_q_up, mla_kv_norm_g, w_kv_up,
              mla_qn_g, mla_kn_g, diff_qn_g, diff_kn_g, lambda_q1, lambda_k1, lambda_q2,
              lambda_k2, diff_subln_g, w_out, ln2_g, w_gate, conv_w, conv_b, w_up, w_down):
    params = dict(ln1_g=ln1_g, w_in=w_in, mla_q_norm_g=mla_q_norm_g, w_q_up=w_q_up,
                  mla_kv_norm_g=mla_kv_norm_g, w_kv_up=w_kv_up, mla_qn_g=mla_qn_g,
                  mla_kn_g=mla_kn_g, diff_qn_g=diff_qn_g, diff_kn_g=diff_kn_g,
                  lambda_q1=lambda_q1, lambda_k1=lambda_k1, lambda_q2=lambda_q2,
                  lambda_k2=lambda_k2, diff_subln_g=diff_subln_g, w_out=w_out, ln2_g=ln2_g,
                  w_gate=w_gate, conv_w=conv_w, conv_b=conv_b, w_up=w_up, w_down=w_down)
    y_prompt = _trunk(x_prompt, params)
    y_sample = _trunk(x_sample, params)
    return (y_prompt, y_sample)
```

```python
import math
from contextlib import ExitStack
import numpy as np
import concourse.bass as bass
import concourse.mybir as mybir
from concourse.bass_utils import run_bass_kernel_spmd

F32 = mybir.dt.float32
BF16 = mybir.dt.bfloat16
AF = mybir.ActivationFunctionType
ALU = mybir.AluOpType
AX = mybir.AxisListType

D_MODEL = 1024
DEPTH = 4
MLA_HEADS = 8
DIFF_HEADS = 4
D_FF = 2816
NJ = D_FF // 128
IN_DIM = 1952
EPS = 1e-6
MLA_THETA = 10000.0
ROPE_THETA = 500000.0
NG = 112
G_LN1, G_QN, G_KVN, G_MQ, G_MK, G_DQ, G_DK, G_SUB, G_LN2, G_CW, G_CB = 0, 8, 10, 11, 12, 13, 14, 15, 16, 24, 90


class Sched:
    def __init__(self, nc):
        self.nc = nc
        self.engs = {'pe': nc.tensor, 'act': nc.scalar, 'dve': nc.vector, 'pool': nc.gpsimd, 'sp': nc.sync}
        self.csem = {e: nc.alloc_semaphore("c_" + e) for e in ('pe', 'act', 'dve', 'pool')}
        self.ccnt = {e: 0 for e in self.csem}
        self.dsem = {}
        self.waited = {e: {} for e in self.engs}
        self.lastw = {}
        self.reads = {}
        self.nops = 0
        self.nwaits = 0

    def _wait(self, e, tok):
        name, sem, val, src = tok
        if self.waited[e].get(name, 0) >= val:
            return
        self.engs[e].wait_ge(sem, val)
        self.waited[e][name] = val
        self.nwaits += 1

    def op(self, e, reads=(), writes=(), dma=None):
        deps = []
        for r in reads:
            t = self.lastw.get(r)
            if t is not None:
                deps.append((t, 'raw'))
        for w in writes:
            t = self.lastw.get(w)
            if t is not None:
                deps.append((t, 'waw'))
            for t in self.reads.get(w, {}).values():
                deps.append((t, 'war'))
        for t, kind in deps:
            if t[3] == e and dma is None:
                if e == 'pe' or kind != 'raw':
                    continue
            self._wait(e, t)

        def post(ins):
            self.nops += 1
            if dma is None:
                self.ccnt[e] += 1
                tok = ("c_" + e, self.csem[e], self.ccnt[e], e)
                ins.then_inc(self.csem[e], 1)
            else:
                d = self.dsem.get(dma)
                if d is None:
                    d = [self.nc.alloc_semaphore("d_%d" % len(self.dsem)), 0]
                    self.dsem[dma] = d
                d[1] += 16
                ins.then_inc(d[0], 16)
                tok = ("d_" + str(dma), d[0], d[1], 'dma')
            for r in reads:
                self.reads.setdefault(r, {})[tok[0]] = tok
            for w in writes:
                self.lastw[w] = tok
                self.reads[w] = {}
            return ins
        return post

    def barrier(self):
        for e in self.engs:
            for k, d in self.dsem.items():
                if d[1] > 0:
                    self._wait(e, ("d_" + str(k), d[0], d[1], 'dma'))
            for k in self.csem:
                if self.ccnt[k] > 0 and k != e:
                    self._wait(e, ("c_" + k, self.csem[k], self.ccnt[k], k))
        self.lastw = {}
        self.reads = {}


def build(S, depth, lam_inits):
    nc = bass.Bass("TRN2", target_bir_lowering=False)
    NT = S // 512
    KB = S // 128
    NQ = S // 512
    NT2 = S // 256

    def din(name, shape, dt=F32):
        return nc.dram_tensor(name, list(shape), dt, kind="ExternalInput").ap()

    x_in = din("x", [S, D_MODEL])
    y_out = nc.dram_tensor("y", [S, D_MODEL], F32, kind="ExternalOutput").ap()
    w_in_d = din("w_in", [depth, D_MODEL, IN_DIM])
    wq_d = din("wq", [depth, 256, 768])
    wkn_d = din("wkn", [depth, 128, 8 * 96])
    wv_d = din("wv", [depth, 128, 512])
    wout_d = din("wout", [depth, D_MODEL, D_MODEL])
    wg_d = din("wg", [depth, D_MODEL, D_FF])
    wu_d = din("wu", [depth, D_MODEL, D_FF])
    wd_d = din("wd", [depth, D_FF, D_MODEL])
    gpack_d = din("gpack", [depth, 128, NG])
    lamv_d = din("lamv", [depth, 128, 4 * 64])
    ident_d = din("ident", [128, 128])
    r96_d = din("r96", [96, 96])
    r128_d = din("r128", [128, 128])
    e32_d = din("e32", [32, 96])
    tabm_d = din("tabm", [96, 2, S])
    tabd_d = din("tabd", [128, 2, S])

    xTa = nc.dram_tensor("xTa", [D_MODEL, S], F32).ap()
    xTb = nc.dram_tensor("xTb", [D_MODEL, S], F32).ap()
    qm = nc.dram_tensor("qm", [8, 96, S], BF16).ap()
    km = nc.dram_tensor("km", [8, 96, S], BF16).ap()
    vm = nc.dram_tensor("vm", [S, 512], BF16).ap()
    qd = nc.dram_tensor("qd", [4, 128, S], BF16).ap()
    kd = nc.dram_tensor("kd", [4, 128, S], BF16).ap()
    vd = nc.dram_tensor("vd", [S, 512], BF16).ap()
    oT = nc.dram_tensor("oT", [D_MODEL, S], BF16).ap()
    h2d = nc.dram_tensor("h2d", [D_MODEL, S + 2], BF16).ap()

    sc = Sched(nc)
    uctr = [0]

    def un(name):
        uctr[0] += 1
        return "%s_%d" % (name, uctr[0])

    op = sc.op
    PE, ACT, DVE, POOL, SP = nc.tensor, nc.scalar, nc.vector, nc.gpsimd, nc.sync

    ps = nc.alloc_psum_tensor("ps", [128, 8, 512], F32).ap()

    ident = nc.alloc_sbuf_tensor("s_ident", [128, 128], F32).ap()
    ones_b = nc.alloc_sbuf_tensor("s_ones_b", [128, 128], BF16).ap()
    bd64_b = nc.alloc_sbuf_tensor("s_bd64_b", [128, 128], BF16).ap()
    r96_b = nc.alloc_sbuf_tensor("s_r96_b", [96, 96], BF16).ap()
    r128_b = nc.alloc_sbuf_tensor("s_r128_b", [128, 128], BF16).ap()
    e32_b = nc.alloc_sbuf_tensor("s_e32_b", [32, 96], BF16).ap()
    gp = nc.alloc_sbuf_tensor("s_gp", [128, NG], F32).ap()
    nlam = nc.alloc_sbuf_tensor("s_nlam", [128, 1], F32).ap()
    cst = nc.alloc_sbuf_tensor("s_cst", [128, 128], F32).ap()
    epsb = nc.alloc_sbuf_tensor("s_epsb", [128, 1], F32).ap()
    zcol = nc.alloc_sbuf_tensor("s_zcol", [128, 8], BF16).ap()

    op('sp', writes=['ident'], dma='c0')(SP.dma_start(out=ident, in_=ident_d))
    op('pool', writes=['ones_b'])(POOL.memset(ones_b, 1.0))
    op('pool', writes=['epsb'])(POOL.memset(epsb, EPS))
    op('pool', writes=['zcol'])(POOL.memset(zcol, 0.0))
    op('pool', writes=['bd64_b'])(POOL.memset(bd64_b, 0.0))
    op('pool', writes=['bd64_b'])(POOL.memset(bd64_b[0:64, 0:64], 1.0))
    op('pool', writes=['bd64_b'])(POOL.memset(bd64_b[64:128, 64:128], 1.0))
    op('sp', writes=['cst'], dma='c1')(SP.dma_start(out=cst[0:96, 0:96], in_=r96_d))
    op('dve', reads=['cst'], writes=['r96_b'])(DVE.tensor_copy(out=r96_b, in_=cst[0:96, 0:96]))
    op('sp', reads=[], writes=['cst'], dma='c1')(SP.dma_start(out=cst, in_=r128_d))
    op('dve', reads=['cst'], writes=['r128_b'])(DVE.tensor_copy(out=r128_b, in_=cst))
    op('sp', writes=['cst'], dma='c1')(SP.dma_start(out=cst[0:32, 0:96], in_=e32_d))
    op('dve', reads=['cst'], writes=['e32_b'])(DVE.tensor_copy(out=e32_b, in_=cst[0:32, 0:96]))
    for c in range(8):
        op('sp', reads=['zcol'], dma='c2')(SP.dma_start(out=h2d[c * 128:(c + 1) * 128, 0:1], in_=zcol[:, 0:1], allow_slow_non_contiguous=True))
        op('sp', reads=['zcol'], dma='c3')(SP.dma_start(out=h2d[c * 128:(c + 1) * 128, S + 1:S + 2], in_=zcol[:, 1:2], allow_slow_non_contiguous=True))
    sc.barrier()

    bank_ctr = [0]

    def nb():
        b = bank_ctr[0] % 8
        bank_ctr[0] += 1
        return b

    def pb(b):
        return 'ps%d' % b

    def load_w(stage, dst, src, ncols, key, scale=None, eng_i=[0]):
        P = dst.shape[0]
        sl = eng_i[0] % 2
        eng_i[0] += 1
        st = stage[0:P, sl, 0:ncols]
        op('sp', writes=['stage%d' % sl], dma='wst%d' % sl)(SP.dma_start(out=st, in_=src))
        e = ('pool', 'dve')[sl]
        E = (POOL, DVE)[sl]
        if scale is None:
            op(e, reads=['stage%d' % sl], writes=[key])(E.tensor_copy(out=dst, in_=st))
        else:
            op(e, reads=['stage%d' % sl], writes=[key])(E.tensor_scalar(out=dst, in0=st, scalar1=float(scale), scalar2=None, op0=ALU.mult))

    with ExitStack() as _st:
        xin_h = _st.enter_context(nc.sbuf_tensor(un("p0_xin"), [128, 2, 4, D_MODEL], F32))
        xt_h = _st.enter_context(nc.sbuf_tensor(un("p0_xt"), [128, 2, 8, 512], F32))
        xin = xin_h.ap()
        xt = xt_h.ap()
        for i in range(NT):
            sl = i % 2
            op('sp', writes=['xin%d' % sl], dma='p0l%d' % sl)(
                SP.dma_start(out=xin[:, sl], in_=x_in[i * 512:(i + 1) * 512, :].rearrange("(s p) f -> p s f", p=128)))
            for c in range(8):
                b = nb()
                for s in range(4):
                    op('pe', reads=['xin%d' % sl, 'ident'], writes=[pb(b)])(
                        PE.transpose(ps[:, b, s * 128:(s + 1) * 128], xin[:, sl, s, c * 128:(c + 1) * 128], ident))
                if c % 2 == 0:
                    op('dve', reads=[pb(b)], writes=['xt%d.%d' % (sl, c)])(DVE.tensor_copy(out=xt[:, sl, c, :], in_=ps[:, b, :]))
                else:
                    op('act', reads=[pb(b)], writes=['xt%d.%d' % (sl, c)])(ACT.copy(out=xt[:, sl, c, :], in_=ps[:, b, :]))
            op('sp', reads=['xt%d.%d' % (sl, c) for c in range(8)], dma='p0s%d' % sl)(
                SP.dma_start(out=xTa[:, i * 512:(i + 1) * 512].rearrange("(c p) t -> p c t", p=128), in_=xt[:, sl]))
    sc.barrier()

    for l in range(depth):
        lam_init = lam_inits[l]
        with ExitStack() as _st:
            lamv_h = _st.enter_context(nc.sbuf_tensor(un("s_lamv"), [128, 4, 64], F32))
            lamt_h = _st.enter_context(nc.sbuf_tensor(un("s_lamt"), [128, 8], F32))
            lamv = lamv_h.ap()
            lamt = lamt_h.ap()
            op('sp', writes=['gp'], dma='c0')(SP.dma_start(out=gp, in_=gpack_d[l]))
            op('sp', writes=['lamv'], dma='c1')(SP.dma_start(out=lamv, in_=lamv_d[l].rearrange("p (a b) -> p a b", a=4)))
            op('dve', reads=['lamv'], writes=['lamv'])(DVE.tensor_tensor(out=lamv[:, 0, :], in0=lamv[:, 0, :], in1=lamv[:, 1, :], op=ALU.mult))
            op('dve', reads=['lamv'], writes=['lamv'])(DVE.tensor_tensor(out=lamv[:, 2, :], in0=lamv[:, 2, :], in1=lamv[:, 3, :], op=ALU.mult))
            op('dve', reads=['lamv'], writes=['lamt'])(DVE.reduce_sum(out=lamt[:, 0:1], in_=lamv[:, 0, :], axis=AX.X))
            op('dve', reads=['lamv', 'lamt'], writes=['lamt'])(DVE.reduce_sum(out=lamt[:, 1:2], in_=lamv[:, 2, :], axis=AX.X))
            op('act', reads=['lamt'], writes=['lamt2'])(ACT.activation(out=lamt[:, 2:4], in_=lamt[:, 0:2], func=AF.Exp))
            op('dve', reads=['lamt2'], writes=['lamt3'])(DVE.tensor_tensor(out=lamt[:, 4:5], in0=lamt[:, 3:4], in1=lamt[:, 2:3], op=ALU.subtract))
            op('dve', reads=['lamt3'], writes=['nlam'])(DVE.tensor_scalar(out=nlam, in0=lamt[:, 4:5], scalar1=-float(lam_init), scalar2=None, op0=ALU.add))
            sc.barrier()

        with ExitStack() as _st:
            win_h = _st.enter_context(nc.sbuf_tensor(un("a_win"), [128, 8, IN_DIM], BF16))
            wq_h = _st.enter_context(nc.sbuf_tensor(un("a_wq"), [128, 2, 768], BF16))
            wkn_h = _st.enter_context(nc.sbuf_tensor(un("a_wkn"), [128, 8, 96], BF16))
            wv_h = _st.enter_context(nc.sbuf_tensor(un("a_wv"), [128, 512], BF16))
            stg_h = _st.enter_context(nc.sbuf_tensor(un("a_stage"), [128, 2, IN_DIM], F32))
            xT_h = _st.enter_context(nc.sbuf_tensor(un("a_xT"), [128, 2, 8, 512], F32))
            tm_h = _st.enter_context(nc.sbuf_tensor(un("a_tm"), [96, 2, 2, 512], F32))
            td_h = _st.enter_context(nc.sbuf_tensor(un("a_td"), [128, 2, 2, 512], F32))
            sq_h = _st.enter_context(nc.sbuf_tensor(un("a_sq"), [128, 8, 512], BF16))
            hT_h = _st.enter_context(nc.sbuf_tensor(un("a_hT"), [128, 8, 512], BF16))
            rs_h = _st.enter_context(nc.sbuf_tensor(un("a_rs"), [128, 2, 512], F32))
            cqn_h = _st.enter_context(nc.sbuf_tensor(un("a_cqn"), [128, 2, 512], BF16))
            ckvn_h = _st.enter_context(nc.sbuf_tensor(un("a_ckvn"), [128, 512], BF16))
            kpe_h = _st.enter_context(nc.sbuf_tensor(un("a_kpe"), [32, 512], BF16))
            hsq_h = _st.enter_context(nc.sbuf_tensor(un("a_hsq"), [128, 4, 512], BF16))
            hrs_h = _st.enter_context(nc.sbuf_tensor(un("a_hrs"), [128, 3, 512], F32))
            hzn_h = _st.enter_context(nc.sbuf_tensor(un("a_hzn"), [128, 4, 512], BF16))
            ht1_h = _st.enter_context(nc.sbuf_tensor(un("a_ht1"), [128, 4, 512], F32))
            ht2_h = _st.enter_context(nc.sbuf_tensor(un("a_ht2"), [128, 2, 512], F32))
            hzf_h = _st.enter_context(nc.sbuf_tensor(un("a_hzf"), [128, 3, 512], BF16))
            vst_h = _st.enter_context(nc.sbuf_tensor(un("a_vst"), [128, 2, 4, 512], BF16))
            win, wq, wkn, wv, stg = win_h.ap(), wq_h.ap(), wkn_h.ap(), wv_h.ap(), stg_h.ap()
            xT, tm, td, sq, hT, rs = xT_h.ap(), tm_h.ap(), td_h.ap(), sq_h.ap(), hT_h.ap(), rs_h.ap()
            cqn, ckvn, kpe = cqn_h.ap(), ckvn_h.ap(), kpe_h.ap()
            hsq, hrs, hzn, ht1, ht2, hzf, vst = hsq_h.ap(), hrs_h.ap(), hzn_h.ap(), ht1_h.ap(), ht2_h.ap(), hzf_h.ap(), vst_h.ap()

            for c in range(8):
                load_w(stg, win[:, c, :], w_in_d[l, c * 128:(c + 1) * 128, :], IN_DIM, 'win')
            for c in range(2):
                load_w(stg, wq[:, c, :], wq_d[l, c * 128:(c + 1) * 128, :], 768, 'wq')
            load_w(stg, wkn.rearrange("p a b -> p (a b)"), wkn_d[l], 768, 'wkn')
            load_w(stg, wv, wv_d[l], 512, 'wv')

            hk = [0]

            def head_stage1(ht):
                ht['k'] = hk[0]
                hk[0] += 1
                k = ht['k']
                D = ht['D']
                b = k % 4
                ht['zb'] = b
                ht['zemit'](b)
                op('act', reads=[pb(b)], writes=['hsq%d' % (k % 4)])(ACT.activation(out=hsq[0:D, k % 4, :], in_=ps[0:D, b, :], func=AF.Square))

            def head_stage2a(ht):
                k, D = ht['k'], ht['D']
                b = 4 + k % 2
                op('pe', reads=['hsq%d' % (k % 4), ht['nmk']], writes=[pb(b)])(
                    PE.matmul(ps[0:D, b, :], lhsT=ht['nm'], rhs=hsq[0:D, k % 4, :], start=True, stop=True))
                op('act', reads=[pb(b), 'epsb'], writes=['hrs%d' % (k % 3)])(
                    ACT.activation(out=hrs[0:D, k % 3, :], in_=ps[0:D, b, :], func=AF.Ln, bias=epsb[0:D, :], scale=1.0 / ht['n']))
                op('act', reads=['hrs%d' % (k % 3)], writes=['hrs%d' % (k % 3)])(
                    ACT.activation(out=hrs[0:D, k % 3, :], in_=hrs[0:D, k % 3, :], func=AF.Exp, scale=-0.5))

            def head_stage2b(ht):
                k, D, zb = ht['k'], ht['D'], ht['zb']
                op('dve', reads=[pb(zb), 'hrs%d' % (k % 3), 'gp'], writes=['hzn%d' % (k % 4)])(
                    DVE.scalar_tensor_tensor(out=hzn[0:D, k % 4, :], in0=ps[0:D, zb, :], scalar=ht['g'], in1=hrs[0:D, k % 3, :],
                                             op0=ALU.mult, op1=ALU.mult))
                op('pool', reads=['hzn%d' % (k % 4), ht['tabk']], writes=['ht1%d' % (k % 4)])(
                    POOL.tensor_tensor(out=ht1[0:D, k % 4, :], in0=hzn[0:D, k % 4, :], in1=ht['C'], op=ALU.mult))

            def head_stage3(ht):
                k, D = ht['k'], ht['D']
                b = 6 + k % 2
                op('pe', reads=['hzn%d' % (k % 4), ht['rk']], writes=[pb(b)])(
                    PE.matmul(ps[0:D, b, :], lhsT=ht['R'], rhs=hzn[0:D, k % 4, :], start=True, stop=True))
                op('dve', reads=[pb(b), ht['tabk']], writes=['ht2%d' % (k % 2)])(
                    DVE.tensor_tensor(out=ht2[0:D, k % 2, :], in0=ps[0:D, b, :], in1=ht['S'], op=ALU.mult))
                op('dve', reads=['ht1%d' % (k % 4), 'ht2%d' % (k % 2)], writes=['hzf%d' % (k % 3)])(
                    DVE.tensor_tensor(out=hzf[0:D, k % 3, :], in0=ht1[0:D, k % 4, :], in1=ht2[0:D, k % 2, :], op=ALU.add))
                op('sp', reads=['hzf%d' % (k % 3)], dma='hst%d' % (k % 3))(SP.dma_start(out=ht['dst'], in_=hzf[0:D, k % 3, :]))

            def load_tile(i):
                sl = i % 2
                t0 = i * 512
                op('sp', writes=['xT%d' % sl], dma='axl%d' % sl)(
                    SP.dma_start(out=xT[:, sl], in_=xTa[:, t0:t0 + 512].rearrange("(c p) t -> p c t", p=128)))
                op('sp', writes=['tab%d' % sl], dma='atl%d' % sl)(SP.dma_start(out=tm[:, sl], in_=tabm_d[:, :, t0:t0 + 512]))
                op('sp', writes=['tab%d' % sl], dma='atl%d' % sl)(SP.dma_start(out=td[:, sl], in_=tabd_d[:, :, t0:t0 + 512]))

            load_tile(0)
            for i in range(NT):
                sl = i % 2
                t0 = i * 512
                if i + 1 < NT:
                    load_tile(i + 1)
                xs = 'xT%d' % sl
                tabk = 'tab%d' % sl
                for c in range(8):
                    op('act', reads=[xs], writes=['sq.%d' % c])(ACT.activation(out=sq[:, c, :], in_=xT[:, sl, c, :], func=AF.Square))
                b = nb()
                for c in range(8):
                    op('pe', reads=['sq.%d' % c, 'ones_b'], writes=[pb(b)])(
                        PE.matmul(ps[:, b, :], lhsT=ones_b, rhs=sq[:, c, :], start=(c == 0), stop=(c == 7)))
                op('act', reads=[pb(b), 'epsb'], writes=['rs0'])(
                    ACT.activation(out=rs[:, 0, :], in_=ps[:, b, :], func=AF.Ln, bias=epsb, scale=1.0 / D_MODEL))
                op('act', reads=['rs0'], writes=['rs0'])(ACT.activation(out=rs[:, 0, :], in_=rs[:, 0, :], func=AF.Exp, scale=-0.5))
                for c in range(8):
                    e, E = ('dve', DVE)
                    op(e, reads=[xs, 'rs0', 'gp'], writes=['hT.%d' % c])(
                        E.scalar_tensor_tensor(out=hT[:, c, :], in0=xT[:, sl, c, :], scalar=gp[:, G_LN1 + c:G_LN1 + c + 1],
                                               in1=rs[:, 0, :], op0=ALU.mult, op1=ALU.mult))

                def proj(b, c0, M):
                    for c in range(8):
                        op('pe', reads=['hT.%d' % c, 'win'], writes=[pb(b)])(
                            PE.matmul(ps[0:M, b, :], lhsT=win[:, c, c0:c0 + M], rhs=hT[:, c, :], start=(c == 0), stop=(c == 7)))

                bq = [nb(), nb()]
                for j in range(2):
                    proj(bq[j], 128 * j, 128)
                    op('act', reads=[pb(bq[j])], writes=['sq.%d' % j])(ACT.activation(out=sq[:, j, :], in_=ps[:, bq[j], :], func=AF.Square))
                b = nb()
                for j in range(2):
                    op('pe', reads=['sq.%d' % j, 'ones_b'], writes=[pb(b)])(
                        PE.matmul(ps[:, b, :], lhsT=ones_b, rhs=sq[:, j, :], start=(j == 0), stop=(j == 1)))
                op('act', reads=[pb(b), 'epsb'], writes=['rs1'])(
                    ACT.activation(out=rs[:, 1, :], in_=ps[:, b, :], func=AF.Ln, bias=epsb, scale=1.0 / 256))
                op('act', reads=['rs1'], writes=['rs1'])(ACT.activation(out=rs[:, 1, :], in_=rs[:, 1, :], func=AF.Exp, scale=-0.5))
                for j in range(2):
                    op('dve', reads=[pb(bq[j]), 'rs1', 'gp'], writes=['cqn.%d' % j])(
                        DVE.scalar_tensor_tensor(out=cqn[:, j, :], in0=ps[:, bq[j], :], scalar=gp[:, G_QN + j:G_QN + j + 1],
                                                 in1=rs[:, 1, :], op0=ALU.mult, op1=ALU.mult))
                bkv = nb()
                proj(bkv, 256, 128)
                op('act', reads=[pb(bkv)], writes=['sq.2'])(ACT.activation(out=sq[:, 2, :], in_=ps[:, bkv, :], func=AF.Square))
                b = nb()
                op('pe', reads=['sq.2', 'ones_b'], writes=[pb(b)])(PE.matmul(ps[:, b, :], lhsT=ones_b, rhs=sq[:, 2, :], start=True, stop=True))
                op('act', reads=[pb(b), 'epsb'], writes=['rs0'])(
                    ACT.activation(out=rs[:, 0, :], in_=ps[:, b, :], func=AF.Ln, bias=epsb, scale=1.0 / 128))
                op('act', reads=['rs0'], writes=['rs0'])(ACT.activation(out=rs[:, 0, :], in_=rs[:, 0, :], func=AF.Exp, scale=-0.5))
                op('dve', reads=[pb(bkv), 'rs0', 'gp'], writes=['ckvn'])(
                    DVE.scalar_tensor_tensor(out=ckvn, in0=ps[:, bkv, :], scalar=gp[:, G_KVN:G_KVN + 1], in1=rs[:, 0, :],
                                             op0=ALU.mult, op1=ALU.mult))
                b = nb()
                proj(b, 384, 32)
                op('act', reads=[pb(b)], writes=['kpe'])(ACT.copy(out=kpe, in_=ps[0:32, b, :]))

                hts = []
                for h in range(8):
                    def zq(b, h=h):
                        for j in range(2):
                            op('pe', reads=['cqn.%d' % j, 'wq'], writes=[pb(b)])(
                                PE.matmul(ps[0:96, b, :], lhsT=wq[:, j, 96 * h:96 * h + 96], rhs=cqn[:, j, :], start=(j == 0), stop=(j == 1)))
                    hts.append(dict(D=96, n=96, zemit=zq, nm=ones_b[0:96, 0:96], nmk='ones_b', g=gp[0:96, G_MQ:G_MQ + 1],
                                    R=r96_b, rk='r96_b', C=tm[:, sl, 0, :], S=tm[:, sl, 1, :], tabk=tabk, dst=qm[h, :, t0:t0 + 512]))
                for h in range(8):
                    def zk(b, h=h):
                        op('pe', reads=['ckvn', 'wkn'], writes=[pb(b)])(
                            PE.matmul(ps[0:96, b, :], lhsT=wkn[:, h, :], rhs=ckvn, start=True, stop=False))
                        op('pe', reads=['kpe', 'e32_b'], writes=[pb(b)])(
                            PE.matmul(ps[0:96, b, :], lhsT=e32_b, rhs=kpe, start=False, stop=True))
                    hts.append(dict(D=96, n=96, zemit=zk, nm=ones_b[0:96, 0:96], nmk='ones_b', g=gp[0:96, G_MK:G_MK + 1],
                                    R=r96_b, rk='r96_b', C=tm[:, sl, 0, :], S=tm[:, sl, 1, :], tabk=tabk, dst=km[h, :, t0:t0 + 512]))
                for hh in range(4):
                    def zdq(b, hh=hh):
                        proj(b, 416 + 128 * hh, 128)
                    hts.append(dict(D=128, n=64, zemit=zdq, nm=bd64_b, nmk='bd64_b', g=gp[:, G_DQ:G_DQ + 1],
                                    R=r128_b, rk='r128_b', C=td[:, sl, 0, :], S=td[:, sl, 1, :], tabk=tabk, dst=qd[hh, :, t0:t0 + 512]))
                for hh in range(4):
                    def zdk(b, hh=hh):
                        proj(b, 928 + 128 * hh, 128)
                    hts.append(dict(D=128, n=64, zemit=zdk, nm=bd64_b, nmk='bd64_b', g=gp[:, G_DK:G_DK + 1],
                                    R=r128_b, rk='r128_b', C=td[:, sl, 0, :], S=td[:, sl, 1, :], tabk=tabk, dst=kd[hh, :, t0:t0 + 512]))
                n = len(hts)
                for k in range(n + 5):
                    if 0 <= k - 5 < n:
                        head_stage3(hts[k - 5])
                    if 0 <= k - 3 < n:
                        head_stage2b(hts[k - 3])
                    if 0 <= k - 2 < n:
                        head_stage2a(hts[k - 2])
                    if k < n:
                        head_stage1(hts[k])

                vs = 'vst%d' % sl
                for s in range(4):
                    b = nb()
                    op('pe', reads=['ckvn', 'wv'], writes=[pb(b)])(
                        PE.matmul(ps[:, b, :], lhsT=ckvn[:, s * 128:(s + 1) * 128], rhs=wv, start=True, stop=True))
                    op('act', reads=[pb(b)], writes=[vs + 'm'])(ACT.copy(out=vst[:, 0, s, :], in_=ps[:, b, :]))
                op('sp', reads=[vs + 'm'], dma='avm%d' % sl)(
                    SP.dma_start(out=vm[t0:t0 + 512, :].rearrange("(s p) f -> p s f", p=128), in_=vst[:, 0]))
                for s in range(4):
                    b = nb()
                    for c in range(8):
                        op('pe', reads=['hT.%d' % c, 'win'], writes=[pb(b)])(
                            PE.matmul(ps[:, b, :], lhsT=hT[:, c, s * 128:(s + 1) * 128], rhs=win[:, c, 1440:1952], start=(c == 0), stop=(c == 7)))
                    op('dve', reads=[pb(b)], writes=[vs + 'd'])(DVE.tensor_copy(out=vst[:, 1, s, :], in_=ps[:, b, :]))
                op('sp', reads=[vs + 'd'], dma='avd%d' % sl)(
                    SP.dma_start(out=vd[t0:t0 + 512, :].rearrange("(s p) f -> p s f", p=128), in_=vst[:, 1]))
        sc.barrier()

        with ExitStack() as _st:
            bq_h = _st.enter_context(nc.sbuf_tensor(un("b_q"), [128, 2, S], BF16))
            bk_h = _st.enter_context(nc.sbuf_tensor(un("b_k"), [128, 2, S], BF16))
            bv_h = _st.enter_context(nc.sbuf_tensor(un("b_v"), [128, 2, KB, 128], BF16))
            bp_h = _st.enter_context(nc.sbuf_tensor(un("b_p"), [128, 3, 3, 512], BF16))
            be_h = _st.enter_context(nc.sbuf_tensor(un("b_e"), [128, 6, 512], F32))
            bsq_h = _st.enter_context(nc.sbuf_tensor(un("b_sq"), [128, 512], BF16))
            bo_h = _st.enter_context(nc.sbuf_tensor(un("b_o"), [128, 2, 512], BF16))
            qb_, kb_, vb_, pbuf, eb, bsq, ob = bq_h.ap(), bk_h.ap(), bv_h.ap(), bp_h.ap(), be_h.ap(), bsq_h.ap(), bo_h.ap()
            accs = _st.enter_context(nc.sbuf_tensor(un("b_accs"), [128, 512], F32)).ap()
            acch = _st.enter_context(nc.sbuf_tensor(un("b_acch"), [128, 2, 512], BF16)).ap()

            heads = [('m', h) for h in range(8)] + [('d', hh) for hh in range(4)]

            def load_head(idx):
                kind, h = heads[idx]
                sl = idx % 2
                if kind == 'm':
                    if idx < 2:
                        op('pool', writes=['V%d' % sl])(POOL.memset(vb_[:, sl, :, 64:128], 1.0))
                    op('sp', writes=['Q%d' % sl], dma='bql%d' % sl)(SP.dma_start(out=qb_[0:96, sl, :], in_=qm[h]))
                    op('sp', writes=['K%d' % sl], dma='bkl%d' % sl)(SP.dma_start(out=kb_[0:96, sl, :], in_=km[h]))
                    for k0 in range(0, KB, 8):
                        op('sp', writes=['V%d' % sl], dma='bvl%d' % sl)(
                            SP.dma_start(out=vb_[:, sl, k0:k0 + 8, 0:64],
                                         in_=vm[k0 * 128:(k0 + 8) * 128, h * 64:(h + 1) * 64].rearrange("(kb p) d -> p kb d", p=128)))
                else:
                    op('sp', writes=['Q%d' % sl], dma='bql%d' % sl)(SP.dma_start(out=qb_[:, sl, :], in_=qd[h]))
                    op('sp', writes=['K%d' % sl], dma='bkl%d' % sl)(SP.dma_start(out=kb_[:, sl, :], in_=kd[h]))
                    for k0 in range(0, KB, 8):
                        op('sp', writes=['V%d' % sl], dma='bvl%d' % sl)(
                            SP.dma_start(out=vb_[:, sl, k0:k0 + 8, :],
                                         in_=vd[k0 * 128:(k0 + 8) * 128, h * 128:(h + 1) * 128].rearrange("(kb p) d -> p kb d", p=128)))

            load_head(0)
            gctr = [0]
            for idx, (kind, h) in enumerate(heads):
                sl = idx % 2
                if idx + 1 < len(heads):
                    load_head(idx + 1)
                Qk, Kk, Vk = 'Q%d' % sl, 'K%d' % sl, 'V%d' % sl
                if kind == 'm':
                    G = 3
                    tiles = [(0, kb) for kb in range(KB)]
                    rows = [(0, 96)]
                    scale = 96 ** -0.5
                    sbase = [0, 3]
                else:
                    G = 2
                    tiles = [(m, kb) for kb in range(KB) for m in (0, 1)]
                    rows = [(0, 64), (64, 128)]
                    scale = 64 ** -0.5
                    sbase = [0, 2]
                groups = [tiles[a:a + G] for a in range(0, len(tiles), G)]
                for qi in range(NQ):
                    q0 = qi * 512
                    if kind == 'm':
                        accb = [6 + (qi % 2)]
                        sumb = None
                    else:
                        accb = [4, 5]
                        sumb = [6, 7]

                    def emit_qk(gi):
                        sb = sbase[gi % 2]
                        for j, (m, kb) in enumerate(groups[gi]):
                            r0, r1 = rows[m]
                            op('pe', reads=[Qk, Kk], writes=[pb(sb + j)])(
                                PE.matmul(ps[:, sb + j, :], lhsT=kb_[r0:r1, sl, kb * 128:(kb + 1) * 128], rhs=qb_[r0:r1, sl, q0:q0 + 512],
                                          start=True, stop=True))

                    def emit_exp(gi, pslot):
                        sb = sbase[gi % 2]
                        ng = len(groups[gi])
                        op('act', reads=[pb(sb + j) for j in range(ng)], writes=['P%d' % pslot])(
                            ACT.activation(out=pbuf[:, pslot, 0:ng, :], in_=ps[:, sb:sb + ng, :], func=AF.Exp, scale=float(scale)))

                    def emit_pv(gi, pslot):
                        for j, (m, kb) in enumerate(groups[gi]):
                            first = (kb == 0)
                            last = (kb == KB - 1)
                            op('pe', reads=['P%d' % pslot, Vk], writes=[pb(accb[m])])(
                                PE.matmul(ps[:, accb[m], :], lhsT=vb_[:, sl, kb, :], rhs=pbuf[:, pslot, j, :], start=first, stop=last))
                            if sumb is not None and m == 1:
                                op('pe', reads=['P%d' % pslot, 'ones_b'], writes=[pb(sumb[m])])(
                                    PE.matmul(ps[:, sumb[m], :], lhsT=ones_b, rhs=pbuf[:, pslot, j, :], start=first, stop=last))
                            if sumb is not None and m == 0:
                                if first:
                                    op('pool', reads=['P%d' % pslot], writes=['accs'])(POOL.tensor_copy(out=accs, in_=pbuf[:, pslot, j, :]))
                                else:
                                    op('pool', reads=['P%d' % pslot, 'accs'], writes=['accs'])(
                                        POOL.tensor_tensor(out=accs, in0=accs, in1=pbuf[:, pslot, j, :], op=ALU.add))

                    ng_ = len(groups)
                    emit_qk(0)
                    if ng_ > 1:
                        emit_qk(1)
                    for gi in range(ng_):
                        pslot = gctr[0] % 3
                        gctr[0] += 1
                        emit_exp(gi, pslot)
                        if gi + 2 < ng_:
                            emit_qk(gi + 2)
                        emit_pv(gi, pslot)
                    if kind == 'd':
                        op('pool', reads=['accs'], writes=['acch'])(POOL.tensor_copy(out=acch[:, 0, :], in_=accs))
                        op('pool', reads=['accs', 'acch'], writes=['accl'])(
                            POOL.tensor_tensor(out=acch[:, 1, :], in0=accs, in1=acch[:, 0, :], op=ALU.subtract))
                        op('pe', reads=['acch', 'ones_b'], writes=[pb(6)])(
                            PE.matmul(ps[:, 6, :], lhsT=ones_b, rhs=acch[:, 0, :], start=True, stop=False))
                        op('pe', reads=['accl', 'ones_b'], writes=[pb(6)])(
                            PE.matmul(ps[:, 6, :], lhsT=ones_b, rhs=acch[:, 1, :], start=False, stop=True))

                    osl = qi % 2
                    if kind == 'm':
                        a = accb[0]
                        op('dve', reads=[pb(a)], writes=['e0'])(DVE.reciprocal(out=eb[0:64, 0, :], in_=ps[64:128, a, :]))
                        op('dve', reads=[pb(a), 'e0'], writes=['o%d' % osl])(
                            DVE.tensor_tensor(out=ob[0:64, osl, :], in0=ps[0:64, a, :], in1=eb[0:64, 0, :], op=ALU.mult))
                        op('sp', reads=['o%d' % osl], dma='bos%d' % osl)(
                            SP.dma_start(out=oT[h * 64:(h + 1) * 64, q0:q0 + 512], in_=ob[0:64, osl, :]))
                    else:
                        op('dve', reads=[pb(6)], writes=['e0'])(DVE.tensor_copy(out=eb[:, 0, :], in_=ps[:, 6, :]))
                        op('dve', reads=[pb(7)], writes=['e1'])(DVE.tensor_copy(out=eb[:, 1, :], in_=ps[:, 7, :]))
                        op('dve', reads=[pb(4)], writes=['e2'])(DVE.tensor_copy(out=eb[:, 2, :], in_=ps[:, 4, :]))
                        op('dve', reads=[pb(5)], writes=['e3'])(DVE.tensor_copy(out=eb[:, 3, :], in_=ps[:, 5, :]))
                        op('dve', reads=['e0'], writes=['e0'])(DVE.reciprocal(out=eb[:, 0, :], in_=eb[:, 0, :]))
                        op('dve', reads=['e1'], writes=['e1'])(DVE.reciprocal(out=eb[:, 1, :], in_=eb[:, 1, :]))
                        op('dve', reads=['e2', 'e0'], writes=['e2'])(DVE.tensor_tensor(out=eb[:, 2, :], in0=eb[:, 2, :], in1=eb[:, 0, :], op=ALU.mult))
                        op('dve', reads=['e3', 'e1'], writes=['e3'])(DVE.tensor_tensor(out=eb[:, 3, :], in0=eb[:, 3, :], in1=eb[:, 1, :], op=ALU.mult))
                        op('dve', reads=['e2', 'e3', 'nlam'], writes=['o%d' % osl])(
                            DVE.scalar_tensor_tensor(out=ob[:, osl, :], in0=eb[:, 3, :], scalar=nlam[:, 0:1], in1=eb[:, 2, :], op0=ALU.mult, op1=ALU.add))
                        op('sp', reads=['o%d' % osl], dma='bos%d' % osl)(
                            SP.dma_start(out=oT[512 + h * 128:512 + (h + 1) * 128, q0:q0 + 512], in_=ob[:, osl, :]))
        sc.barrier()

        with ExitStack() as _st:
            wout_h = _st.enter_context(nc.sbuf_tensor(un("c1_wout"), [128, 8, D_MODEL], BF16))
            stg_h = _st.enter_context(nc.sbuf_tensor(un("c1_stage"), [128, 2, D_MODEL], F32))
            o_h = _st.enter_context(nc.sbuf_tensor(un("c1_oT"), [128, 2, 8, 512], BF16))
            xT_h = _st.enter_context(nc.sbuf_tensor(un("c1_xT"), [128, 2, 8, 512], F32))
            sq_h = _st.enter_context(nc.sbuf_tensor(un("c1_sq"), [128, 8, 512], BF16))
            h2_h = _st.enter_context(nc.sbuf_tensor(un("c1_h2"), [128, 2, 8, 512], BF16))
            rs_h = _st.enter_context(nc.sbuf_tensor(un("c1_rs"), [128, 512], F32))
            sqd = _st.enter_context(nc.sbuf_tensor(un("c1_sqd"), [128, 4, 512], BF16)).ap()
            rsd = _st.enter_context(nc.sbuf_tensor(un("c1_rsd"), [128, 4, 512], F32)).ap()
            wout, stg, ot, xT, sq, h2, rs = wout_h.ap(), stg_h.ap(), o_h.ap(), xT_h.ap(), sq_h.ap(), h2_h.ap(), rs_h.ap()
            for c in range(8):
                load_w(stg, wout[:, c, :], wout_d[l, c * 128:(c + 1) * 128, :], D_MODEL, 'wout',
                       scale=(None if c < 4 else (1.0 - lam_init)))

            def load_c1(i):
                sl = i % 2
                t0 = i * 512
                op('sp', writes=['ot%d.%d' % (sl, c) for c in range(8)], dma='c1o%d' % sl)(
                    SP.dma_start(out=ot[:, sl], in_=oT[:, t0:t0 + 512].rearrange("(c p) t -> p c t", p=128)))
                op('sp', writes=['x%d.%d' % (sl, c) for c in range(8)], dma='c1x%d' % sl)(
                    SP.dma_start(out=xT[:, sl], in_=xTa[:, t0:t0 + 512].rearrange("(c p) t -> p c t", p=128)))

            load_c1(0)
            for i in range(NT):
                sl = i % 2
                t0 = i * 512
                if i + 1 < NT:
                    load_c1(i + 1)
                for hh in range(4):
                    c = 4 + hh
                    ok = 'ot%d.%d' % (sl, c)
                    op('act', reads=[ok], writes=['sqd.%d' % hh])(ACT.activation(out=sqd[:, hh, :], in_=ot[:, sl, c, :], func=AF.Square))
                    b = nb()
                    op('pe', reads=['sqd.%d' % hh, 'ones_b'], writes=[pb(b)])(PE.matmul(ps[:, b, :], lhsT=ones_b, rhs=sqd[:, hh, :], start=True, stop=True))
                    op('act', reads=[pb(b), 'epsb'], writes=['rsd.%d' % hh])(
                        ACT.activation(out=rsd[:, hh, :], in_=ps[:, b, :], func=AF.Ln, bias=epsb, scale=1.0 / 128))
                    op('act', reads=['rsd.%d' % hh], writes=['rsd.%d' % hh])(ACT.activation(out=rsd[:, hh, :], in_=rsd[:, hh, :], func=AF.Exp, scale=-0.5))
                    op('dve', reads=[ok, 'rsd.%d' % hh, 'gp'], writes=[ok])(
                        DVE.scalar_tensor_tensor(out=ot[:, sl, c, :], in0=ot[:, sl, c, :], scalar=gp[:, G_SUB:G_SUB + 1], in1=rsd[:, hh, :],
                                                 op0=ALU.mult, op1=ALU.mult))
                for m in range(8):
                    b = nb()
                    for c in range(8):
                        op('pe', reads=['ot%d.%d' % (sl, c), 'wout'], writes=[pb(b)])(
                            PE.matmul(ps[:, b, :], lhsT=wout[:, c, m * 128:(m + 1) * 128], rhs=ot[:, sl, c, :], start=(c == 0), stop=(c == 7)))
                    xk = 'x%d.%d' % (sl, m)
                    op('dve', reads=[pb(b), xk], writes=[xk])(
                        DVE.tensor_tensor(out=xT[:, sl, m, :], in0=ps[:, b, :], in1=xT[:, sl, m, :], op=ALU.add))
                    op('act', reads=[xk], writes=['sq.%d' % m])(ACT.activation(out=sq[:, m, :], in_=xT[:, sl, m, :], func=AF.Square))
                op('sp', reads=['x%d.%d' % (sl, c) for c in range(8)], dma='c1xs%d' % sl)(
                    SP.dma_start(out=xTb[:, t0:t0 + 512].rearrange("(c p) t -> p c t", p=128), in_=xT[:, sl]))
                b = nb()
                for c in range(8):
                    op('pe', reads=['sq.%d' % c, 'ones_b'], writes=[pb(b)])(
                        PE.matmul(ps[:, b, :], lhsT=ones_b, rhs=sq[:, c, :], start=(c == 0), stop=(c == 7)))
                op('act', reads=[pb(b), 'epsb'], writes=['rs'])(
                    ACT.activation(out=rs, in_=ps[:, b, :], func=AF.Ln, bias=epsb, scale=1.0 / D_MODEL))
                op('act', reads=['rs'], writes=['rs'])(ACT.activation(out=rs, in_=rs, func=AF.Exp, scale=-0.5))
                for c in range(8):
                    e, E = ('dve', DVE)
                    op(e, reads=['x%d.%d' % (sl, c), 'rs', 'gp'], writes=['h2%d' % sl])(
                        E.scalar_tensor_tensor(out=h2[:, sl, c, :], in0=xT[:, sl, c, :], scalar=gp[:, G_LN2 + c:G_LN2 + c + 1],
                                               in1=rs, op0=ALU.mult, op1=ALU.mult))
                op('sp', reads=['h2%d' % sl], dma='c1hs%d' % sl)(
                    SP.dma_start(out=h2d[:, 1 + t0:1 + t0 + 512].rearrange("(c p) t -> p c t", p=128), in_=h2[:, sl]))
        sc.barrier()

        with ExitStack() as _st:
            wg_h = _st.enter_context(nc.sbuf_tensor(un("c2_wg"), [128, 8, D_FF], BF16))
            wu_h = _st.enter_context(nc.sbuf_tensor(un("c2_wu"), [128, 8, D_FF], BF16))
            wd_h = _st.enter_context(nc.sbuf_tensor(un("c2_wd"), [128, NJ, D_MODEL], BF16))
            wg, wu, wd = wg_h.ap(), wu_h.ap(), wd_h.ap()
            with ExitStack() as _st:
                stg_h = _st.enter_context(nc.sbuf_tensor(un("c2_stage"), [128, 2, D_FF], F32))
                stg = stg_h.ap()
                for c in range(8):
                    load_w(stg, wg[:, c, :], wg_d[l, c * 128:(c + 1) * 128, :], D_FF, 'wg.%d' % c)
                    load_w(stg, wu[:, c, :], wu_d[l, c * 128:(c + 1) * 128, :], D_FF, 'wu.%d' % c)
                for j in range(NJ):
                    load_w(stg, wd[:, j, :], wd_d[l, j * 128:(j + 1) * 128, :], D_MODEL, 'wd.%d' % j)
                sc.barrier()
            with ExitStack() as _st:
                TW = 510
                h2 = _st.enter_context(nc.sbuf_tensor(un("c2_h2"), [128, 2, 8, TW + 2], BF16)).ap()
                x1 = _st.enter_context(nc.sbuf_tensor(un("c2_x1"), [128, 3, TW], F32)).ap()
                gb = _st.enter_context(nc.sbuf_tensor(un("c2_g"), [128, 2, TW], F32)).ap()
                act = _st.enter_context(nc.sbuf_tensor(un("c2_act"), [128, NJ, TW], BF16)).ap()
                tiles2 = [(t0, min(TW, S - t0)) for t0 in range(0, S, TW)]

                def load_c2(i):
                    sl = i % 2
                    t0, W = tiles2[i]
                    op('sp', writes=['h2%d' % sl], dma='c2h%d' % sl)(
                        SP.dma_start(out=h2[:, sl, :, 0:W + 2], in_=h2d[:, t0:t0 + W + 2].rearrange("(c p) t -> p c t", p=128)))

                xc = [0]

                def load_x1(i, m):
                    t0, W = tiles2[i]
                    xs_ = (i * 8 + m) % 3
                    op('sp', writes=['x1c%d' % xs_], dma='c2x%d' % xs_)(
                        SP.dma_start(out=x1[:, xs_, 0:W], in_=xTb[m * 128:(m + 1) * 128, t0:t0 + W]))

                load_c2(0)
                jc = [0]
                for i in range(len(tiles2)):
                    sl = i % 2
                    t0, W = tiles2[i]
                    if i + 1 < len(tiles2):
                        load_c2(i + 1)
                    hk_ = 'h2%d' % sl
                    for j in range(NJ):
                        js = jc[0] % 2
                        jc[0] += 1
                        bg = nb()
                        for c in range(8):
                            op('pe', reads=[hk_, 'wg'], writes=[pb(bg)])(
                                PE.matmul(ps[:, bg, 0:W + 2], lhsT=wg[:, c, j * 128:(j + 1) * 128], rhs=h2[:, sl, c, 0:W + 2], start=(c == 0), stop=(c == 7)))
                        bu = nb()
                        for c in range(8):
                            op('pe', reads=[hk_, 'wu'], writes=[pb(bu)])(
                                PE.matmul(ps[:, bu, 0:W], lhsT=wu[:, c, j * 128:(j + 1) * 128], rhs=h2[:, sl, c, 1:W + 1], start=(c == 0), stop=(c == 7)))
                        cw = G_CW + 3 * j
                        op('act', reads=[pb(bg), 'gp'], writes=['g%d' % js])(
                            ACT.activation(out=gb[:, js, 0:W], in_=ps[:, bg, 0:W], func=AF.Identity, bias=gp[:, G_CB + j:G_CB + j + 1],
                                           scale=gp[:, cw:cw + 1]))
                        op('dve', reads=[pb(bg), 'g%d' % js, 'gp'], writes=['g%d' % js])(
                            DVE.scalar_tensor_tensor(out=gb[:, js, 0:W], in0=ps[:, bg, 1:W + 1], scalar=gp[:, cw + 1:cw + 2], in1=gb[:, js, 0:W],
                                                     op0=ALU.mult, op1=ALU.add))
                        op('dve', reads=[pb(bg), 'g%d' % js, 'gp'], writes=['g%d' % js])(
                            DVE.scalar_tensor_tensor(out=gb[:, js, 0:W], in0=ps[:, bg, 2:W + 2], scalar=gp[:, cw + 2:cw + 3], in1=gb[:, js, 0:W],
                                                     op0=ALU.mult, op1=ALU.add))
                        op('act', reads=['g%d' % js], writes=['g%d' % js])(ACT.activation(out=gb[:, js, 0:W], in_=gb[:, js, 0:W], func=AF.Silu))
                        op('dve', reads=['g%d' % js, pb(bu)], writes=['act.%d' % j])(
                            DVE.tensor_tensor(out=act[:, j, 0:W], in0=ps[:, bu, 0:W], in1=gb[:, js, 0:W], op=ALU.mult))
                        if j >= NJ - 3:
                            load_x1(i, j - (NJ - 3))
                    for m in range(8):
                        b = nb()
                        for j in range(NJ):
                            op('pe', reads=['act.%d' % j, 'wd'], writes=[pb(b)])(
                                PE.matmul(ps[:, b, 0:W], lhsT=wd[:, j, m * 128:(m + 1) * 128], rhs=act[:, j, 0:W], start=(j == 0), stop=(j == NJ - 1)))
                        xs_ = (i * 8 + m) % 3
                        xk = 'x1c%d' % xs_
                        op('dve', reads=[pb(b), xk], writes=[xk])(
                            DVE.tensor_tensor(out=x1[:, xs_, 0:W], in0=ps[:, b, 0:W], in1=x1[:, xs_, 0:W], op=ALU.add))
                        op('sp', reads=[xk], dma='c2xs%d' % xs_)(
                            SP.dma_start(out=xTa[m * 128:(m + 1) * 128, t0:t0 + W], in_=x1[:, xs_, 0:W]))
                        if m + 3 < 8:
                            load_x1(i, m + 3)
        sc.barrier()

    with ExitStack() as _st:
        xt_h = _st.enter_context(nc.sbuf_tensor(un("z_xt"), [128, 2, 8, 512], F32))
        y_h = _st.enter_context(nc.sbuf_tensor(un("z_y"), [128, 2, 4, D_MODEL], F32))
        xt, yb = xt_h.ap(), y_h.ap()
        for i in range(NT):
            sl = i % 2
            op('sp', writes=['zx%d' % sl], dma='zl%d' % sl)(
                SP.dma_start(out=xt[:, sl], in_=xTa[:, i * 512:(i + 1) * 512].rearrange("(c p) t -> p c t", p=128)))
            for s in range(4):
                for half in range(2):
                    b = nb()
                    for cc in range(4):
                        c = half * 4 + cc
                        op('pe', reads=['zx%d' % sl, 'ident'], writes=[pb(b)])(
                            PE.transpose(ps[:, b, cc * 128:(cc + 1) * 128], xt[:, sl, c, s * 128:(s + 1) * 128], ident))
                    if half == 0:
                        op('dve', reads=[pb(b)], writes=['zy%d.%d.%d' % (sl, s, half)])(
                            DVE.tensor_copy(out=yb[:, sl, s, half * 512:(half + 1) * 512], in_=ps[:, b, :]))
                    else:
                        op('act', reads=[pb(b)], writes=['zy%d.%d.%d' % (sl, s, half)])(
                            ACT.copy(out=yb[:, sl, s, half * 512:(half + 1) * 512], in_=ps[:, b, :]))
            op('sp', reads=['zy%d.%d.%d' % (sl, s, hf) for s in range(4) for hf in range(2)], dma='zs%d' % sl)(
                SP.dma_start(out=y_out[i * 512:(i + 1) * 512, :].rearrange("(s p) f -> p s f", p=128), in_=yb[:, sl]))
    sc.barrier()
    return nc, sc


def _rope_tables(S):
    pos = np.arange(S, dtype=np.float32)
    fm = (np.float32(MLA_THETA) ** (-np.arange(16, dtype=np.float32) * np.float32(2.0) / np.float32(32))).astype(np.float32)
    angm = (pos[:, None] * fm[None, :]).astype(np.float32)
    tabm = np.zeros((96, 2, S), np.float32)
    tabm[:, 0, :] = 1.0
    tabm[64:80, 0, :] = np.cos(angm).T
    tabm[80:96, 0, :] = np.cos(angm).T
    tabm[64:80, 1, :] = np.sin(angm).T
    tabm[80:96, 1, :] = np.sin(angm).T
    fd = (np.float32(ROPE_THETA) ** (-np.arange(8, dtype=np.float32) * np.float32(2.0) / np.float32(16))).astype(np.float32)
    angd = (pos[:, None] * fd[None, :]).astype(np.float32)
    tabd = np.zeros((128, 2, S), np.float32)
    tabd[:, 0, :] = 1.0
    for m in range(2):
        o = 64 * m
        tabd[o:o + 8, 0, :] = np.cos(angd).T
        tabd[o + 8:o + 16, 0, :] = np.cos(angd).T
        tabd[o:o + 8, 1, :] = np.sin(angd).T
        tabd[o + 8:o + 16, 1, :] = np.sin(angd).T
    return tabm, tabd


def _rot_mats():
    r96 = np.zeros((96, 96), np.float32)
    for i in range(16):
        r96[64 + 16 + i, 64 + i] = -1.0
        r96[64 + i, 64 + 16 + i] = 1.0
    r128 = np.zeros((128, 128), np.float32)
    for m in range(2):
        o = 64 * m
        for i in range(8):
            r128[o + 8 + i, o + i] = -1.0
            r128[o + i, o + 8 + i] = 1.0
    e32 = np.zeros((32, 96), np.float32)
    for i in range(32):
        e32[i, 64 + i] = 1.0
    return r96, r128, e32


def _prep_shared(S, depth, p):
    f = lambda a: np.ascontiguousarray(np.asarray(a, dtype=np.float32))
    w_kv = f(p['w_kv_up'])[:depth].reshape(depth, 128, 8, 128)
    wkn = np.zeros((depth, 128, 8, 96), np.float32)
    wkn[:, :, :, 0:64] = w_kv[:, :, :, 0:64]
    wv = np.ascontiguousarray(w_kv[:, :, :, 64:128]).reshape(depth, 128, 512)
    gpack = np.zeros((depth, 128, NG), np.float32)
    for l in range(depth):
        gpack[l, :, G_LN1:G_LN1 + 8] = f(p['ln1_g'])[l].reshape(8, 128).T
        gpack[l, :, G_QN:G_QN + 2] = f(p['mla_q_norm_g'])[l].reshape(2, 128).T
        gpack[l, :, G_KVN] = f(p['mla_kv_norm_g'])[l]
        gpack[l, 0:96, G_MQ] = f(p['mla_qn_g'])[l]
        gpack[l, 0:96, G_MK] = f(p['mla_kn_g'])[l]
        gpack[l, 0:64, G_DQ] = f(p['diff_qn_g'])[l]
        gpack[l, 64:128, G_DQ] = f(p['diff_qn_g'])[l]
        gpack[l, 0:64, G_DK] = f(p['diff_kn_g'])[l]
        gpack[l, 64:128, G_DK] = f(p['diff_kn_g'])[l]
        gpack[l, :, G_SUB] = f(p['diff_subln_g'])[l]
        gpack[l, :, G_LN2:G_LN2 + 8] = f(p['ln2_g'])[l].reshape(8, 128).T
        cw = f(p['conv_w'])[l]
        for j in range(NJ):
            for k in range(3):
                gpack[l, :, G_CW + 3 * j + k] = cw[k, j * 128:(j + 1) * 128]
        gpack[l, :, G_CB:G_CB + NJ] = f(p['conv_b'])[l].reshape(NJ, 128).T
    lamv = np.zeros((depth, 128, 4, 64), np.float32)
    for l in range(depth):
        for a, nm in enumerate(('lambda_q1', 'lambda_k1', 'lambda_q2', 'lambda_k2')):
            lamv[l, :, a, :] = f(p[nm])[l][None, :]
    tabm, tabd = _rope_tables(S)
    r96, r128, e32 = _rot_mats()
    return {
        "w_in": f(p['w_in'])[:depth], "wq": f(p['w_q_up'])[:depth], "wkn": wkn.reshape(depth, 128, 768), "wv": wv,
        "wout": f(p['w_out'])[:depth], "wg": f(p['w_gate'])[:depth], "wu": f(p['w_up'])[:depth], "wd": f(p['w_down'])[:depth],
        "gpack": gpack, "lamv": lamv.reshape(depth, 128, 256), "ident": np.eye(128, dtype=np.float32),
        "r96": r96, "r128": r128, "e32": e32, "tabm": tabm, "tabd": tabd,
    }


def run_trunk(seqs, params, depth=DEPTH):
    S = seqs[0].shape[0]
    lam_inits = [0.8 - 0.6 * math.exp(-0.3 * l) for l in range(depth)]
    nc, sc = build(S, depth, lam_inits)
    shared = _prep_shared(S, depth, params)
    n = len(seqs)
    in_maps = []
    for c in range(8):
        m = dict(shared)
        m["x"] = np.ascontiguousarray(seqs[c % n], dtype=np.float32)
        in_maps.append(m)
    res = run_bass_kernel_spmd(nc, in_maps, core_ids=list(range(8)))
    return [np.asarray(res.results[c]["y"], dtype=np.float32) for c in range(n)]


def kernel(x_prompt, x_sample, ln1_g, w_in, mla_q_norm_g, w_q_up, mla_kv_norm_g, w_kv_up,
           mla_qn_g, mla_kn_g, diff_qn_g, diff_kn_g, lambda_q1, lambda_k1, lambda_q2,
           lambda_k2, diff_subln_g, w_out, ln2_g, w_gate, conv_w, conv_b, w_up, w_down):
    params = dict(ln1_g=ln1_g, w_in=w_in, mla_q_norm_g=mla_q_norm_g, w_q_up=w_q_up,
                  mla_kv_norm_g=mla_kv_norm_g, w_kv_up=w_kv_up, mla_qn_g=mla_qn_g,
                  mla_kn_g=mla_kn_g, diff_qn_g=diff_qn_g, diff_kn_g=diff_kn_g,
                  lambda_q1=lambda_q1, lambda_k1=lambda_k1, lambda_q2=lambda_q2,
                  lambda_k2=lambda_k2, diff_subln_g=diff_subln_g, w_out=w_out, ln2_g=ln2_g,
                  w_gate=w_gate, conv_w=conv_w, conv_b=conv_b, w_up=w_up, w_down=w_down)
    xp = np.asarray(x_prompt, dtype=np.float32)
    xs = np.asarray(x_sample, dtype=np.float32)
    seqs = [xp[b] for b in range(xp.shape[0])] + [xs[b] for b in range(xs.shape[0])]
    outs = run_trunk(seqs, params, DEPTH)
    y_prompt = np.stack(outs[:xp.shape[0]], axis=0)
    y_sample = np.stack(outs[xp.shape[0]:], axis=0)
    return (y_prompt, y_sample)
```

```python
import math
from contextlib import ExitStack
import numpy as np
import concourse.bass as bass
import concourse.mybir as mybir
from concourse.bass_utils import run_bass_kernel_spmd

F32 = mybir.dt.float32
BF16 = mybir.dt.bfloat16
AF = mybir.ActivationFunctionType
ALU = mybir.AluOpType
AX = mybir.AxisListType

D_MODEL = 1024
DEPTH = 4
MLA_HEADS = 8
DIFF_HEADS = 4
D_FF = 2816
NJ = D_FF // 128
IN_DIM = 1952
EPS = 1e-6
MLA_THETA = 10000.0
ROPE_THETA = 500000.0
NG = 112
G_LN1, G_QN, G_KVN, G_MQ, G_MK, G_DQ, G_DK, G_SUB, G_LN2, G_CW, G_CB = 0, 8, 10, 11, 12, 13, 14, 15, 16, 24, 90


class Sched:
    def __init__(self, nc):
        self.nc = nc
        self.engs = {'pe': nc.tensor, 'act': nc.scalar, 'dve': nc.vector, 'pool': nc.gpsimd, 'sp': nc.sync}
        self.csem = {e: nc.alloc_semaphore("c_" + e) for e in ('pe', 'act', 'dve', 'pool')}
        self.ccnt = {e: 0 for e in self.csem}
        self.dsem = {}
        self.waited = {e: {} for e in self.engs}
        self.lastw = {}
        self.reads = {}
        self.nops = 0
        self.nwaits = 0

    def _wait(self, e, tok):
        name, sem, val, src = tok
        if self.waited[e].get(name, 0) >= val:
            return
        self.engs[e].wait_ge(sem, val)
        self.waited[e][name] = val
        self.nwaits += 1

    def op(self, e, reads=(), writes=(), dma=None):
        deps = []
        for r in reads:
            t = self.lastw.get(r)
            if t is not None:
                deps.append((t, 'raw'))
        for w in writes:
            t = self.lastw.get(w)
            if t is not None:
                deps.append((t, 'waw'))
            for t in self.reads.get(w, {}).values():
                deps.append((t, 'war'))
        for t, kind in deps:
            if t[3] == e and dma is None:
                if e == 'pe' or (kind != 'raw' and e != 'pool'):
                    continue
            self._wait(e, t)

        def post(ins):
            self.nops += 1
            if dma is None:
                self.ccnt[e] += 1
                tok = ("c_" + e, self.csem[e], self.ccnt[e], e)
                ins.then_inc(self.csem[e], 1)
            else:
                d = self.dsem.get(dma)
                if d is None:
                    d = [self.nc.alloc_semaphore("d_%d" % len(self.dsem)), 0]
                    self.dsem[dma] = d
                d[1] += 16
                ins.then_inc(d[0], 16)
                tok = ("d_" + str(dma), d[0], d[1], 'dma')
            for r in reads:
                self.reads.setdefault(r, {})[tok[0]] = tok
            for w in writes:
                self.lastw[w] = tok
                self.reads[w] = {}
            return ins
        return post

    def barrier(self):
        for e in self.engs:
            for k, d in self.dsem.items():
                if d[1] > 0:
                    self._wait(e, ("d_" + str(k), d[0], d[1], 'dma'))
            for k in self.csem:
                if self.ccnt[k] > 0 and k != e:
                    self._wait(e, ("c_" + k, self.csem[k], self.ccnt[k], k))
        self.lastw = {}
        self.reads = {}


def build(S, depth, lam_inits):
    nc = bass.Bass("TRN2", target_bir_lowering=False)
    NT = S // 512
    KB = S // 128
    NQ = S // 512
    NT2 = S // 256

    def din(name, shape, dt=F32):
        return nc.dram_tensor(name, list(shape), dt, kind="ExternalInput").ap()

    x_in = din("x", [S, D_MODEL])
    y_out = nc.dram_tensor("y", [S, D_MODEL], F32, kind="ExternalOutput").ap()
    w_in_d = din("w_in", [depth, D_MODEL, IN_DIM])
    wq_d = din("wq", [depth, 256, 768])
    wkn_d = din("wkn", [depth, 128, 8 * 96])
    wv_d = din("wv", [depth, 128, 512])
    wout_d = din("wout", [depth, D_MODEL, D_MODEL])
    wg_d = din("wg", [depth, D_MODEL, D_FF])
    wu_d = din("wu", [depth, D_MODEL, D_FF])
    wd_d = din("wd", [depth, D_FF, D_MODEL])
    gpack_d = din("gpack", [depth, 128, NG])
    lamv_d = din("lamv", [depth, 128, 4 * 64])
    ident_d = din("ident", [128, 128])
    r96_d = din("r96", [96, 96])
    r128_d = din("r128", [128, 128])
    e32_d = din("e32", [32, 96])
    tabm_d = din("tabm", [96, 2, S])
    tabd_d = din("tabd", [128, 2, S])

    xTa = nc.dram_tensor("xTa", [D_MODEL, S], F32).ap()
    xTb = nc.dram_tensor("xTb", [D_MODEL, S], F32).ap()
    qm = nc.dram_tensor("qm", [8, 96, S], BF16).ap()
    km = nc.dram_tensor("km", [8, 96, S], BF16).ap()
    vm = nc.dram_tensor("vm", [S, 512], BF16).ap()
    qd = nc.dram_tensor("qd", [4, 128, S], BF16).ap()
    kd = nc.dram_tensor("kd", [4, 128, S], BF16).ap()
    vd = nc.dram_tensor("vd", [S, 512], BF16).ap()
    oT = nc.dram_tensor("oT", [D_MODEL, S], BF16).ap()
    h2d = nc.dram_tensor("h2d", [D_MODEL, S + 2], BF16).ap()

    sc = Sched(nc)
    uctr = [0]

    def un(name):
        uctr[0] += 1
        return "%s_%d" % (name, uctr[0])

    op = sc.op
    PE, ACT, DVE, POOL, SP = nc.tensor, nc.scalar, nc.vector, nc.gpsimd, nc.sync

    ps = nc.alloc_psum_tensor("ps", [128, 8, 512], F32).ap()

    ident = nc.alloc_sbuf_tensor("s_ident", [128, 128], F32).ap()
    ones_b = nc.alloc_sbuf_tensor("s_ones_b", [128, 128], BF16).ap()
    bd64_b = nc.alloc_sbuf_tensor("s_bd64_b", [128, 128], BF16).ap()
    r96_b = nc.alloc_sbuf_tensor("s_r96_b", [96, 96], BF16).ap()
    r128_b = nc.alloc_sbuf_tensor("s_r128_b", [128, 128], BF16).ap()
    e32_b = nc.alloc_sbuf_tensor("s_e32_b", [32, 96], BF16).ap()
    gp = nc.alloc_sbuf_tensor("s_gp", [128, NG], F32).ap()
    nlam = nc.alloc_sbuf_tensor("s_nlam", [128, 1], F32).ap()
    cst = nc.alloc_sbuf_tensor("s_cst", [128, 128], F32).ap()
    epsb = nc.alloc_sbuf_tensor("s_epsb", [128, 1], F32).ap()
    zcol = nc.alloc_sbuf_tensor("s_zcol", [128, 8], BF16).ap()

    op('sp', writes=['ident'], dma='c0')(SP.dma_start(out=ident, in_=ident_d))
    op('pool', writes=['ones_b'])(POOL.memset(ones_b, 1.0))
    op('pool', writes=['epsb'])(POOL.memset(epsb, EPS))
    op('pool', writes=['zcol'])(POOL.memset(zcol, 0.0))
    op('pool', writes=['bd64_b'])(POOL.memset(bd64_b, 0.0))
    op('pool', writes=['bd64_b'])(POOL.memset(bd64_b[0:64, 0:64], 1.0))
    op('pool', writes=['bd64_b'])(POOL.memset(bd64_b[64:128, 64:128], 1.0))
    op('sp', writes=['cst'], dma='c1')(SP.dma_start(out=cst[0:96, 0:96], in_=r96_d))
    op('dve', reads=['cst'], writes=['r96_b'])(DVE.tensor_copy(out=r96_b, in_=cst[0:96, 0:96]))
    op('sp', reads=[], writes=['cst'], dma='c1')(SP.dma_start(out=cst, in_=r128_d))
    op('dve', reads=['cst'], writes=['r128_b'])(DVE.tensor_copy(out=r128_b, in_=cst))
    op('sp', writes=['cst'], dma='c1')(SP.dma_start(out=cst[0:32, 0:96], in_=e32_d))
    op('dve', reads=['cst'], writes=['e32_b'])(DVE.tensor_copy(out=e32_b, in_=cst[0:32, 0:96]))
    for c in range(8):
        op('sp', reads=['zcol'], dma='c2')(SP.dma_start(out=h2d[c * 128:(c + 1) * 128, 0:1], in_=zcol[:, 0:1], allow_slow_non_contiguous=True))
        op('sp', reads=['zcol'], dma='c3')(SP.dma_start(out=h2d[c * 128:(c + 1) * 128, S + 1:S + 2], in_=zcol[:, 1:2], allow_slow_non_contiguous=True))
    sc.barrier()

    bank_ctr = [0]

    def nb():
        b = bank_ctr[0] % 8
        bank_ctr[0] += 1
        return b

    def pb(b):
        return 'ps%d' % b

    def load_w(stage, dst, src, ncols, key, scale=None, eng_i=[0]):
        P = dst.shape[0]
        sl = eng_i[0] % 2
        eng_i[0] += 1
        st = stage[0:P, sl, 0:ncols]
        op('sp', writes=['stage%d' % sl], dma='wst%d' % sl)(SP.dma_start(out=st, in_=src))
        e = ('pool', 'dve')[sl]
        E = (POOL, DVE)[sl]
        if scale is None:
            op(e, reads=['stage%d' % sl], writes=[key])(E.tensor_copy(out=dst, in_=st))
        else:
            op(e, reads=['stage%d' % sl], writes=[key])(E.tensor_scalar(out=dst, in0=st, scalar1=float(scale), scalar2=None, op0=ALU.mult))

    with ExitStack() as _st:
        xin_h = _st.enter_context(nc.sbuf_tensor(un("p0_xin"), [128, 2, 4, D_MODEL], F32))
        xt_h = _st.enter_context(nc.sbuf_tensor(un("p0_xt"), [128, 2, 8, 512], F32))
        xin = xin_h.ap()
        xt = xt_h.ap()
        for i in range(NT):
            sl = i % 2
            op('sp', writes=['xin%d' % sl], dma='p0l%d' % sl)(
                SP.dma_start(out=xin[:, sl], in_=x_in[i * 512:(i + 1) * 512, :].rearrange("(s p) f -> p s f", p=128)))
            for c in range(8):
                b = nb()
                for s in range(4):
                    op('pe', reads=['xin%d' % sl, 'ident'], writes=[pb(b)])(
                        PE.transpose(ps[:, b, s * 128:(s + 1) * 128], xin[:, sl, s, c * 128:(c + 1) * 128], ident))
                if c % 2 == 0:
                    op('dve', reads=[pb(b)], writes=['xt%d.%d' % (sl, c)])(DVE.tensor_copy(out=xt[:, sl, c, :], in_=ps[:, b, :]))
                else:
                    op('act', reads=[pb(b)], writes=['xt%d.%d' % (sl, c)])(ACT.copy(out=xt[:, sl, c, :], in_=ps[:, b, :]))
            op('sp', reads=['xt%d.%d' % (sl, c) for c in range(8)], dma='p0s%d' % sl)(
                SP.dma_start(out=xTa[:, i * 512:(i + 1) * 512].rearrange("(c p) t -> p c t", p=128), in_=xt[:, sl]))
    sc.barrier()

    for l in range(depth):
        lam_init = lam_inits[l]
        with ExitStack() as _st:
            lamv_h = _st.enter_context(nc.sbuf_tensor(un("s_lamv"), [128, 4, 64], F32))
            lamt_h = _st.enter_context(nc.sbuf_tensor(un("s_lamt"), [128, 8], F32))
            lamv = lamv_h.ap()
            lamt = lamt_h.ap()
            op('sp', writes=['gp'], dma='c0')(SP.dma_start(out=gp, in_=gpack_d[l]))
            op('sp', writes=['lamv'], dma='c1')(SP.dma_start(out=lamv, in_=lamv_d[l].rearrange("p (a b) -> p a b", a=4)))
            op('dve', reads=['lamv'], writes=['lamv'])(DVE.tensor_tensor(out=lamv[:, 0, :], in0=lamv[:, 0, :], in1=lamv[:, 1, :], op=ALU.mult))
            op('dve', reads=['lamv'], writes=['lamv'])(DVE.tensor_tensor(out=lamv[:, 2, :], in0=lamv[:, 2, :], in1=lamv[:, 3, :], op=ALU.mult))
            op('dve', reads=['lamv'], writes=['lamt'])(DVE.reduce_sum(out=lamt[:, 0:1], in_=lamv[:, 0, :], axis=AX.X))
            op('dve', reads=['lamv', 'lamt'], writes=['lamt'])(DVE.reduce_sum(out=lamt[:, 1:2], in_=lamv[:, 2, :], axis=AX.X))
            op('act', reads=['lamt'], writes=['lamt2'])(ACT.activation(out=lamt[:, 2:4], in_=lamt[:, 0:2], func=AF.Exp))
            op('dve', reads=['lamt2'], writes=['lamt3'])(DVE.tensor_tensor(out=lamt[:, 4:5], in0=lamt[:, 3:4], in1=lamt[:, 2:3], op=ALU.subtract))
            op('dve', reads=['lamt3'], writes=['nlam'])(DVE.tensor_scalar(out=nlam, in0=lamt[:, 4:5], scalar1=-float(lam_init), scalar2=None, op0=ALU.add))
            sc.barrier()

        with ExitStack() as _st:
            win_h = _st.enter_context(nc.sbuf_tensor(un("a_win"), [128, 8, IN_DIM], BF16))
            wq_h = _st.enter_context(nc.sbuf_tensor(un("a_wq"), [128, 2, 768], BF16))
            wkn_h = _st.enter_context(nc.sbuf_tensor(un("a_wkn"), [128, 8, 96], BF16))
            wv_h = _st.enter_context(nc.sbuf_tensor(un("a_wv"), [128, 512], BF16))
            stg_h = _st.enter_context(nc.sbuf_tensor(un("a_stage"), [128, 2, IN_DIM], F32))
            xT_h = _st.enter_context(nc.sbuf_tensor(un("a_xT"), [128, 2, 8, 512], F32))
            tm_h = _st.enter_context(nc.sbuf_tensor(un("a_tm"), [96, 2, 2, 512], F32))
            td_h = _st.enter_context(nc.sbuf_tensor(un("a_td"), [128, 2, 2, 512], F32))
            sq_h = _st.enter_context(nc.sbuf_tensor(un("a_sq"), [128, 8, 512], BF16))
            hT_h = _st.enter_context(nc.sbuf_tensor(un("a_hT"), [128, 8, 512], BF16))
            rs_h = _st.enter_context(nc.sbuf_tensor(un("a_rs"), [128, 2, 512], F32))
            cqn_h = _st.enter_context(nc.sbuf_tensor(un("a_cqn"), [128, 2, 512], BF16))
            ckvn_h = _st.enter_context(nc.sbuf_tensor(un("a_ckvn"), [128, 512], BF16))
            kpe_h = _st.enter_context(nc.sbuf_tensor(un("a_kpe"), [32, 512], BF16))
            hsq_h = _st.enter_context(nc.sbuf_tensor(un("a_hsq"), [128, 4, 512], BF16))
            hrs_h = _st.enter_context(nc.sbuf_tensor(un("a_hrs"), [128, 3, 512], F32))
            hzn_h = _st.enter_context(nc.sbuf_tensor(un("a_hzn"), [128, 4, 512], BF16))
            ht1_h = _st.enter_context(nc.sbuf_tensor(un("a_ht1"), [128, 4, 512], F32))
            ht2_h = _st.enter_context(nc.sbuf_tensor(un("a_ht2"), [128, 2, 512], F32))
            hzf_h = _st.enter_context(nc.sbuf_tensor(un("a_hzf"), [128, 3, 512], BF16))
            vst_h = _st.enter_context(nc.sbuf_tensor(un("a_vst"), [128, 2, 4, 512], BF16))
            win, wq, wkn, wv, stg = win_h.ap(), wq_h.ap(), wkn_h.ap(), wv_h.ap(), stg_h.ap()
            xT, tm, td, sq, hT, rs = xT_h.ap(), tm_h.ap(), td_h.ap(), sq_h.ap(), hT_h.ap(), rs_h.ap()
            cqn, ckvn, kpe = cqn_h.ap(), ckvn_h.ap(), kpe_h.ap()
            hsq, hrs, hzn, ht1, ht2, hzf, vst = hsq_h.ap(), hrs_h.ap(), hzn_h.ap(), ht1_h.ap(), ht2_h.ap(), hzf_h.ap(), vst_h.ap()

            for c in range(8):
                load_w(stg, win[:, c, :], w_in_d[l, c * 128:(c + 1) * 128, :], IN_DIM, 'win')
            for c in range(2):
                load_w(stg, wq[:, c, :], wq_d[l, c * 128:(c + 1) * 128, :], 768, 'wq')
            load_w(stg, wkn.rearrange("p a b -> p (a b)"), wkn_d[l], 768, 'wkn')
            load_w(stg, wv, wv_d[l], 512, 'wv')

            hk = [0]

            def head_stage1(ht):
                ht['k'] = hk[0]
                hk[0] += 1
                k = ht['k']
                D = ht['D']
                b = k % 4
                ht['zb'] = b
                ht['zemit'](b)
                op('act', reads=[pb(b)], writes=['hsq%d' % (k % 4)])(ACT.activation(out=hsq[0:D, k % 4, :], in_=ps[0:D, b, :], func=AF.Square))

            def head_stage2a(ht):
                k, D = ht['k'], ht['D']
                b = 4 + k % 2
                op('pe', reads=['hsq%d' % (k % 4), ht['nmk']], writes=[pb(b)])(
                    PE.matmul(ps[0:D, b, :], lhsT=ht['nm'], rhs=hsq[0:D, k % 4, :], start=True, stop=True))
                op('act', reads=[pb(b), 'epsb'], writes=['hrs%d' % (k % 3)])(
                    ACT.activation(out=hrs[0:D, k % 3, :], in_=ps[0:D, b, :], func=AF.Ln, bias=epsb[0:D, :], scale=1.0 / ht['n']))
                op('act', reads=['hrs%d' % (k % 3)], writes=['hrs%d' % (k % 3)])(
                    ACT.activation(out=hrs[0:D, k % 3, :], in_=hrs[0:D, k % 3, :], func=AF.Exp, scale=-0.5))

            def head_stage2b(ht):
                k, D, zb = ht['k'], ht['D'], ht['zb']
                op('dve', reads=[pb(zb), 'hrs%d' % (k % 3), 'gp'], writes=['hzn%d' % (k % 4)])(
                    DVE.scalar_tensor_tensor(out=hzn[0:D, k % 4, :], in0=ps[0:D, zb, :], scalar=ht['g'], in1=hrs[0:D, k % 3, :],
                                             op0=ALU.mult, op1=ALU.mult))
                op('pool', reads=['hzn%d' % (k % 4), ht['tabk']], writes=['ht1%d' % (k % 4)])(
                    POOL.tensor_tensor(out=ht1[0:D, k % 4, :], in0=hzn[0:D, k % 4, :], in1=ht['C'], op=ALU.mult))

            def head_stage3(ht):
                k, D = ht['k'], ht['D']
                b = 6 + k % 2
                op('pe', reads=['hzn%d' % (k % 4), ht['rk']], writes=[pb(b)])(
                    PE.matmul(ps[0:D, b, :], lhsT=ht['R'], rhs=hzn[0:D, k % 4, :], start=True, stop=True))
                op('dve', reads=[pb(b), ht['tabk']], writes=['ht2%d' % (k % 2)])(
                    DVE.tensor_tensor(out=ht2[0:D, k % 2, :], in0=ps[0:D, b, :], in1=ht['S'], op=ALU.mult))
                if k % 2 == 0:
                    op('dve', reads=['ht1%d' % (k % 4), 'ht2%d' % (k % 2)], writes=['hzf%d' % (k % 3)])(
                        DVE.tensor_tensor(out=hzf[0:D, k % 3, :], in0=ht1[0:D, k % 4, :], in1=ht2[0:D, k % 2, :], op=ALU.add))
                else:
                    op('pool', reads=['ht1%d' % (k % 4), 'ht2%d' % (k % 2)], writes=['hzf%d' % (k % 3)])(
                        POOL.tensor_tensor(out=hzf[0:D, k % 3, :], in0=ht1[0:D, k % 4, :], in1=ht2[0:D, k % 2, :], op=ALU.add))
                op('sp', reads=['hzf%d' % (k % 3)], dma='hst%d' % (k % 3))(SP.dma_start(out=ht['dst'], in_=hzf[0:D, k % 3, :]))

            def load_tile(i):
                sl = i % 2
                t0 = i * 512
                op('sp', writes=['xT%d' % sl], dma='axl%d' % sl)(
                    SP.dma_start(out=xT[:, sl], in_=xTa[:, t0:t0 + 512].rearrange("(c p) t -> p c t", p=128)))
                op('sp', writes=['tab%d' % sl], dma='atl%d' % sl)(SP.dma_start(out=tm[:, sl], in_=tabm_d[:, :, t0:t0 + 512]))
                op('sp', writes=['tab%d' % sl], dma='atl%d' % sl)(SP.dma_start(out=td[:, sl], in_=tabd_d[:, :, t0:t0 + 512]))

            load_tile(0)
            for i in range(NT):
                sl = i % 2
                t0 = i * 512
                if i + 1 < NT:
                    load_tile(i + 1)
                xs = 'xT%d' % sl
                tabk = 'tab%d' % sl
                for c in range(8):
                    op('act', reads=[xs], writes=['sq.%d' % c])(ACT.activation(out=sq[:, c, :], in_=xT[:, sl, c, :], func=AF.Square))
                b = nb()
                for c in range(8):
                    op('pe', reads=['sq.%d' % c, 'ones_b'], writes=[pb(b)])(
                        PE.matmul(ps[:, b, :], lhsT=ones_b, rhs=sq[:, c, :], start=(c == 0), stop=(c == 7)))
                op('act', reads=[pb(b), 'epsb'], writes=['rs0'])(
                    ACT.activation(out=rs[:, 0, :], in_=ps[:, b, :], func=AF.Ln, bias=epsb, scale=1.0 / D_MODEL))
                op('act', reads=['rs0'], writes=['rs0'])(ACT.activation(out=rs[:, 0, :], in_=rs[:, 0, :], func=AF.Exp, scale=-0.5))
                for c in range(8):
                    e, E = ('dve', DVE)
                    op(e, reads=[xs, 'rs0', 'gp'], writes=['hT.%d' % c])(
                        E.scalar_tensor_tensor(out=hT[:, c, :], in0=xT[:, sl, c, :], scalar=gp[:, G_LN1 + c:G_LN1 + c + 1],
                                               in1=rs[:, 0, :], op0=ALU.mult, op1=ALU.mult))

                def proj(b, c0, M):
                    for c in range(8):
                        op('pe', reads=['hT.%d' % c, 'win'], writes=[pb(b)])(
                            PE.matmul(ps[0:M, b, :], lhsT=win[:, c, c0:c0 + M], rhs=hT[:, c, :], start=(c == 0), stop=(c == 7)))

                bq = [nb(), nb()]
                for j in range(2):
                    proj(bq[j], 128 * j, 128)
                    op('act', reads=[pb(bq[j])], writes=['sq.%d' % j])(ACT.activation(out=sq[:, j, :], in_=ps[:, bq[j], :], func=AF.Square))
                b = nb()
                for j in range(2):
                    op('pe', reads=['sq.%d' % j, 'ones_b'], writes=[pb(b)])(
                        PE.matmul(ps[:, b, :], lhsT=ones_b, rhs=sq[:, j, :], start=(j == 0), stop=(j == 1)))
                op('act', reads=[pb(b), 'epsb'], writes=['rs1'])(
                    ACT.activation(out=rs[:, 1, :], in_=ps[:, b, :], func=AF.Ln, bias=epsb, scale=1.0 / 256))
                op('act', reads=['rs1'], writes=['rs1'])(ACT.activation(out=rs[:, 1, :], in_=rs[:, 1, :], func=AF.Exp, scale=-0.5))
                for j in range(2):
                    op('dve', reads=[pb(bq[j]), 'rs1', 'gp'], writes=['cqn.%d' % j])(
                        DVE.scalar_tensor_tensor(out=cqn[:, j, :], in0=ps[:, bq[j], :], scalar=gp[:, G_QN + j:G_QN + j + 1],
                                                 in1=rs[:, 1, :], op0=ALU.mult, op1=ALU.mult))
                bkv = nb()
                proj(bkv, 256, 128)
                op('act', reads=[pb(bkv)], writes=['sq.2'])(ACT.activation(out=sq[:, 2, :], in_=ps[:, bkv, :], func=AF.Square))
                b = nb()
                op('pe', reads=['sq.2', 'ones_b'], writes=[pb(b)])(PE.matmul(ps[:, b, :], lhsT=ones_b, rhs=sq[:, 2, :], start=True, stop=True))
                op('act', reads=[pb(b), 'epsb'], writes=['rs0'])(
                    ACT.activation(out=rs[:, 0, :], in_=ps[:, b, :], func=AF.Ln, bias=epsb, scale=1.0 / 128))
                op('act', reads=['rs0'], writes=['rs0'])(ACT.activation(out=rs[:, 0, :], in_=rs[:, 0, :], func=AF.Exp, scale=-0.5))
                op('dve', reads=[pb(bkv), 'rs0', 'gp'], writes=['ckvn'])(
                    DVE.scalar_tensor_tensor(out=ckvn, in0=ps[:, bkv, :], scalar=gp[:, G_KVN:G_KVN + 1], in1=rs[:, 0, :],
                                             op0=ALU.mult, op1=ALU.mult))
                b = nb()
                proj(b, 384, 32)
                op('act', reads=[pb(b)], writes=['kpe'])(ACT.copy(out=kpe, in_=ps[0:32, b, :]))

                hts = []
                for h in range(8):
                    def zq(b, h=h):
                        for j in range(2):
                            op('pe', reads=['cqn.%d' % j, 'wq'], writes=[pb(b)])(
                                PE.matmul(ps[0:96, b, :], lhsT=wq[:, j, 96 * h:96 * h + 96], rhs=cqn[:, j, :], start=(j == 0), stop=(j == 1)))
                    hts.append(dict(D=96, n=96, zemit=zq, nm=ones_b[0:96, 0:96], nmk='ones_b', g=gp[0:96, G_MQ:G_MQ + 1],
                                    R=r96_b, rk='r96_b', C=tm[:, sl, 0, :], S=tm[:, sl, 1, :], tabk=tabk, dst=qm[h, :, t0:t0 + 512]))
                for h in range(8):
                    def zk(b, h=h):
                        op('pe', reads=['ckvn', 'wkn'], writes=[pb(b)])(
                            PE.matmul(ps[0:96, b, :], lhsT=wkn[:, h, :], rhs=ckvn, start=True, stop=False))
                        op('pe', reads=['kpe', 'e32_b'], writes=[pb(b)])(
                            PE.matmul(ps[0:96, b, :], lhsT=e32_b, rhs=kpe, start=False, stop=True))
                    hts.append(dict(D=96, n=96, zemit=zk, nm=ones_b[0:96, 0:96], nmk='ones_b', g=gp[0:96, G_MK:G_MK + 1],
                                    R=r96_b, rk='r96_b', C=tm[:, sl, 0, :], S=tm[:, sl, 1, :], tabk=tabk, dst=km[h, :, t0:t0 + 512]))
                for hh in range(4):
                    def zdq(b, hh=hh):
                        proj(b, 416 + 128 * hh, 128)
                    hts.append(dict(D=128, n=64, zemit=zdq, nm=bd64_b, nmk='bd64_b', g=gp[:, G_DQ:G_DQ + 1],
                                    R=r128_b, rk='r128_b', C=td[:, sl, 0, :], S=td[:, sl, 1, :], tabk=tabk, dst=qd[hh, :, t0:t0 + 512]))
                for hh in range(4):
                    def zdk(b, hh=hh):
                        proj(b, 928 + 128 * hh, 128)
                    hts.append(dict(D=128, n=64, zemit=zdk, nm=bd64_b, nmk='bd64_b', g=gp[:, G_DK:G_DK + 1],
                                    R=r128_b, rk='r128_b', C=td[:, sl, 0, :], S=td[:, sl, 1, :], tabk=tabk, dst=kd[hh, :, t0:t0 + 512]))
                n = len(hts)
                for k in range(n + 5):
                    if 0 <= k - 5 < n:
                        head_stage3(hts[k - 5])
                    if 0 <= k - 3 < n:
                        head_stage2b(hts[k - 3])
                    if 0 <= k - 2 < n:
                        head_stage2a(hts[k - 2])
                    if k < n:
                        head_stage1(hts[k])

                vs = 'vst%d' % sl
                for s in range(4):
                    b = nb()
                    op('pe', reads=['ckvn', 'wv'], writes=[pb(b)])(
                        PE.matmul(ps[:, b, :], lhsT=ckvn[:, s * 128:(s + 1) * 128], rhs=wv, start=True, stop=True))
                    op('act', reads=[pb(b)], writes=[vs + 'm'])(ACT.copy(out=vst[:, 0, s, :], in_=ps[:, b, :]))
                op('sp', reads=[vs + 'm'], dma='avm%d' % sl)(
                    SP.dma_start(out=vm[t0:t0 + 512, :].rearrange("(s p) f -> p s f", p=128), in_=vst[:, 0]))
                for s in range(4):
                    b = nb()
                    for c in range(8):
                        op('pe', reads=['hT.%d' % c, 'win'], writes=[pb(b)])(
                            PE.matmul(ps[:, b, :], lhsT=hT[:, c, s * 128:(s + 1) * 128], rhs=win[:, c, 1440:1952], start=(c == 0), stop=(c == 7)))
                    op('dve', reads=[pb(b)], writes=[vs + 'd'])(DVE.tensor_copy(out=vst[:, 1, s, :], in_=ps[:, b, :]))
                op('sp', reads=[vs + 'd'], dma='avd%d' % sl)(
                    SP.dma_start(out=vd[t0:t0 + 512, :].rearrange("(s p) f -> p s f", p=128), in_=vst[:, 1]))
        sc.barrier()

        with ExitStack() as _st:
            bq_h = _st.enter_context(nc.sbuf_tensor(un("b_q"), [128, 2, S], BF16))
            bk_h = _st.enter_context(nc.sbuf_tensor(un("b_k"), [128, 2, S], BF16))
            bv_h = _st.enter_context(nc.sbuf_tensor(un("b_v"), [128, 2, KB, 128], BF16))
            bp_h = _st.enter_context(nc.sbuf_tensor(un("b_p"), [128, 3, 3, 512], BF16))
            be_h = _st.enter_context(nc.sbuf_tensor(un("b_e"), [128, 6, 512], F32))
            bsq_h = _st.enter_context(nc.sbuf_tensor(un("b_sq"), [128, 512], BF16))
            bo_h = _st.enter_context(nc.sbuf_tensor(un("b_o"), [128, 2, 512], BF16))
            qb_, kb_, vb_, pbuf, eb, bsq, ob = bq_h.ap(), bk_h.ap(), bv_h.ap(), bp_h.ap(), be_h.ap(), bsq_h.ap(), bo_h.ap()

            heads = [('m', h) for h in range(8)] + [('d', hh) for hh in range(4)]

            def load_head(idx):
                kind, h = heads[idx]
                sl = idx % 2
                if kind == 'm':
                    if idx < 2:
                        op('pool', writes=['V%d' % sl])(POOL.memset(vb_[:, sl, :, 64:128], 1.0))
                    op('sp', writes=['Q%d' % sl], dma='bql%d' % sl)(SP.dma_start(out=qb_[0:96, sl, :], in_=qm[h]))
                    op('sp', writes=['K%d' % sl], dma='bkl%d' % sl)(SP.dma_start(out=kb_[0:96, sl, :], in_=km[h]))
                    for k0 in range(0, KB, 8):
                        op('sp', writes=['V%d' % sl], dma='bvl%d' % sl)(
                            SP.dma_start(out=vb_[:, sl, k0:k0 + 8, 0:64],
                                         in_=vm[k0 * 128:(k0 + 8) * 128, h * 64:(h + 1) * 64].rearrange("(kb p) d -> p kb d", p=128)))
                else:
                    op('sp', writes=['Q%d' % sl], dma='bql%d' % sl)(SP.dma_start(out=qb_[:, sl, :], in_=qd[h]))
                    op('sp', writes=['K%d' % sl], dma='bkl%d' % sl)(SP.dma_start(out=kb_[:, sl, :], in_=kd[h]))
                    for k0 in range(0, KB, 8):
                        op('sp', writes=['V%d' % sl], dma='bvl%d' % sl)(
                            SP.dma_start(out=vb_[:, sl, k0:k0 + 8, :],
                                         in_=vd[k0 * 128:(k0 + 8) * 128, h * 128:(h + 1) * 128].rearrange("(kb p) d -> p kb d", p=128)))

            load_head(0)
            gctr = [0]
            for idx, (kind, h) in enumerate(heads):
                sl = idx % 2
                if idx + 1 < len(heads):
                    load_head(idx + 1)
                Qk, Kk, Vk = 'Q%d' % sl, 'K%d' % sl, 'V%d' % sl
                if kind == 'm':
                    G = 3
                    tiles = [(0, kb) for kb in range(KB)]
                    rows = [(0, 96)]
                    scale = 96 ** -0.5
                    sbase = [0, 3]
                else:
                    G = 2
                    tiles = [(m, kb) for kb in range(KB) for m in (0, 1)]
                    rows = [(0, 64), (64, 128)]
                    scale = 64 ** -0.5
                    sbase = [0, 2]
                groups = [tiles[a:a + G] for a in range(0, len(tiles), G)]
                for qi in range(NQ):
                    q0 = qi * 512
                    if kind == 'm':
                        accb = [6 + (qi % 2)]
                        sumb = None
                    else:
                        accb = [4, 5]
                        sumb = [6, 7]

                    def emit_qk(gi):
                        sb = sbase[gi % 2]
                        for j, (m, kb) in enumerate(groups[gi]):
                            r0, r1 = rows[m]
                            op('pe', reads=[Qk, Kk], writes=[pb(sb + j)])(
                                PE.matmul(ps[:, sb + j, :], lhsT=kb_[r0:r1, sl, kb * 128:(kb + 1) * 128], rhs=qb_[r0:r1, sl, q0:q0 + 512],
                                          start=True, stop=True))

                    def emit_exp(gi, pslot):
                        sb = sbase[gi % 2]
                        ng = len(groups[gi])
                        op('act', reads=[pb(sb + j) for j in range(ng)], writes=['P%d' % pslot])(
                            ACT.activation(out=pbuf[:, pslot, 0:ng, :], in_=ps[:, sb:sb + ng, :], func=AF.Exp, scale=float(scale)))

                    def emit_pv(gi, pslot):
                        for j, (m, kb) in enumerate(groups[gi]):
                            first = (kb == 0)
                            last = (kb == KB - 1)
                            op('pe', reads=['P%d' % pslot, Vk], writes=[pb(accb[m])])(
                                PE.matmul(ps[:, accb[m], :], lhsT=vb_[:, sl, kb, :], rhs=pbuf[:, pslot, j, :], start=first, stop=last))
                            if sumb is not None:
                                op('pe', reads=['P%d' % pslot, 'ones_b'], writes=[pb(sumb[m])])(
                                    PE.matmul(ps[:, sumb[m], :], lhsT=ones_b, rhs=pbuf[:, pslot, j, :], start=first, stop=last))

                    ng_ = len(groups)
                    emit_qk(0)
                    if ng_ > 1:
                        emit_qk(1)
                    for gi in range(ng_):
                        pslot = gctr[0] % 3
                        gctr[0] += 1
                        emit_exp(gi, pslot)
                        if gi + 2 < ng_:
                            emit_qk(gi + 2)
                        emit_pv(gi, pslot)

                    osl = qi % 2
                    if kind == 'm':
                        a = accb[0]
                        op('dve', reads=[pb(a)], writes=['e0'])(DVE.reciprocal(out=eb[0:64, 0, :], in_=ps[64:128, a, :]))
                        op('dve', reads=[pb(a), 'e0'], writes=['o%d' % osl])(
                            DVE.tensor_tensor(out=ob[0:64, osl, :], in0=ps[0:64, a, :], in1=eb[0:64, 0, :], op=ALU.mult))
                        op('sp', reads=['o%d' % osl], dma='bos%d' % osl)(
                            SP.dma_start(out=oT[h * 64:(h + 1) * 64, q0:q0 + 512], in_=ob[0:64, osl, :]))
                    else:
                        op('dve', reads=[pb(6)], writes=['e0'])(DVE.tensor_copy(out=eb[:, 0, :], in_=ps[:, 6, :]))
                        op('dve', reads=[pb(7)], writes=['e1'])(DVE.tensor_copy(out=eb[:, 1, :], in_=ps[:, 7, :]))
                        op('dve', reads=[pb(4)], writes=['e2'])(DVE.tensor_copy(out=eb[:, 2, :], in_=ps[:, 4, :]))
                        op('dve', reads=[pb(5)], writes=['e3'])(DVE.tensor_copy(out=eb[:, 3, :], in_=ps[:, 5, :]))
                        op('dve', reads=['e0'], writes=['e0'])(DVE.reciprocal(out=eb[:, 0, :], in_=eb[:, 0, :]))
                        op('dve', reads=['e1'], writes=['e1'])(DVE.reciprocal(out=eb[:, 1, :], in_=eb[:, 1, :]))
                        op('dve', reads=['e2', 'e0'], writes=['e2'])(DVE.tensor_tensor(out=eb[:, 2, :], in0=eb[:, 2, :], in1=eb[:, 0, :], op=ALU.mult))
                        op('dve', reads=['e3', 'e1'], writes=['e3'])(DVE.tensor_tensor(out=eb[:, 3, :], in0=eb[:, 3, :], in1=eb[:, 1, :], op=ALU.mult))
                        op('dve', reads=['e2', 'e3', 'nlam'], writes=['o%d' % osl])(
                            DVE.scalar_tensor_tensor(out=ob[:, osl, :], in0=eb[:, 3, :], scalar=nlam[:, 0:1], in1=eb[:, 2, :], op0=ALU.mult, op1=ALU.add))
                        op('sp', reads=['o%d' % osl], dma='bos%d' % osl)(
                            SP.dma_start(out=oT[512 + h * 128:512 + (h + 1) * 128, q0:q0 + 512], in_=ob[:, osl, :]))
        sc.barrier()

        with ExitStack() as _st:
            wout_h = _st.enter_context(nc.sbuf_tensor(un("c1_wout"), [128, 8, D_MODEL], BF16))
            stg_h = _st.enter_context(nc.sbuf_tensor(un("c1_stage"), [128, 2, D_MODEL], F32))
            o_h = _st.enter_context(nc.sbuf_tensor(un("c1_oT"), [128, 2, 8, 512], BF16))
            xT_h = _st.enter_context(nc.sbuf_tensor(un("c1_xT"), [128, 2, 8, 512], F32))
            sq_h = _st.enter_context(nc.sbuf_tensor(un("c1_sq"), [128, 8, 512], BF16))
            h2_h = _st.enter_context(nc.sbuf_tensor(un("c1_h2"), [128, 2, 8, 512], BF16))
            rs_h = _st.enter_context(nc.sbuf_tensor(un("c1_rs"), [128, 512], F32))
            sqd = _st.enter_context(nc.sbuf_tensor(un("c1_sqd"), [128, 4, 512], BF16)).ap()
            rsd = _st.enter_context(nc.sbuf_tensor(un("c1_rsd"), [128, 4, 512], F32)).ap()
            wout, stg, ot, xT, sq, h2, rs = wout_h.ap(), stg_h.ap(), o_h.ap(), xT_h.ap(), sq_h.ap(), h2_h.ap(), rs_h.ap()
            for c in range(8):
                load_w(stg, wout[:, c, :], wout_d[l, c * 128:(c + 1) * 128, :], D_MODEL, 'wout',
                       scale=(None if c < 4 else (1.0 - lam_init)))

            def load_c1(i):
                sl = i % 2
                t0 = i * 512
                op('sp', writes=['ot%d.%d' % (sl, c) for c in range(8)], dma='c1o%d' % sl)(
                    SP.dma_start(out=ot[:, sl], in_=oT[:, t0:t0 + 512].rearrange("(c p) t -> p c t", p=128)))
                op('sp', writes=['x%d.%d' % (sl, c) for c in range(8)], dma='c1x%d' % sl)(
                    SP.dma_start(out=xT[:, sl], in_=xTa[:, t0:t0 + 512].rearrange("(c p) t -> p c t", p=128)))

            load_c1(0)
            for i in range(NT):
                sl = i % 2
                t0 = i * 512
                if i + 1 < NT:
                    load_c1(i + 1)
                for hh in range(4):
                    c = 4 + hh
                    ok = 'ot%d.%d' % (sl, c)
                    op('act', reads=[ok], writes=['sqd.%d' % hh])(ACT.activation(out=sqd[:, hh, :], in_=ot[:, sl, c, :], func=AF.Square))
                    b = nb()
                    op('pe', reads=['sqd.%d' % hh, 'ones_b'], writes=[pb(b)])(PE.matmul(ps[:, b, :], lhsT=ones_b, rhs=sqd[:, hh, :], start=True, stop=True))
                    op('act', reads=[pb(b), 'epsb'], writes=['rsd.%d' % hh])(
                        ACT.activation(out=rsd[:, hh, :], in_=ps[:, b, :], func=AF.Ln, bias=epsb, scale=1.0 / 128))
                    op('act', reads=['rsd.%d' % hh], writes=['rsd.%d' % hh])(ACT.activation(out=rsd[:, hh, :], in_=rsd[:, hh, :], func=AF.Exp, scale=-0.5))
                    op('dve', reads=[ok, 'rsd.%d' % hh, 'gp'], writes=[ok])(
                        DVE.scalar_tensor_tensor(out=ot[:, sl, c, :], in0=ot[:, sl, c, :], scalar=gp[:, G_SUB:G_SUB + 1], in1=rsd[:, hh, :],
                                                 op0=ALU.mult, op1=ALU.mult))
                for m in range(8):
                    b = nb()
                    for c in range(8):
                        op('pe', reads=['ot%d.%d' % (sl, c), 'wout'], writes=[pb(b)])(
                            PE.matmul(ps[:, b, :], lhsT=wout[:, c, m * 128:(m + 1) * 128], rhs=ot[:, sl, c, :], start=(c == 0), stop=(c == 7)))
                    xk = 'x%d.%d' % (sl, m)
                    op('dve', reads=[pb(b), xk], writes=[xk])(
                        DVE.tensor_tensor(out=xT[:, sl, m, :], in0=ps[:, b, :], in1=xT[:, sl, m, :], op=ALU.add))
                    op('act', reads=[xk], writes=['sq.%d' % m])(ACT.activation(out=sq[:, m, :], in_=xT[:, sl, m, :], func=AF.Square))
                op('sp', reads=['x%d.%d' % (sl, c) for c in range(8)], dma='c1xs%d' % sl)(
                    SP.dma_start(out=xTb[:, t0:t0 + 512].rearrange("(c p) t -> p c t", p=128), in_=xT[:, sl]))
                b = nb()
                for c in range(8):
                    op('pe', reads=['sq.%d' % c, 'ones_b'], writes=[pb(b)])(
                        PE.matmul(ps[:, b, :], lhsT=ones_b, rhs=sq[:, c, :], start=(c == 0), stop=(c == 7)))
                op('act', reads=[pb(b), 'epsb'], writes=['rs'])(
                    ACT.activation(out=rs, in_=ps[:, b, :], func=AF.Ln, bias=epsb, scale=1.0 / D_MODEL))
                op('act', reads=['rs'], writes=['rs'])(ACT.activation(out=rs, in_=rs, func=AF.Exp, scale=-0.5))
                for c in range(8):
                    e, E = ('dve', DVE)
                    op(e, reads=['x%d.%d' % (sl, c), 'rs', 'gp'], writes=['h2%d' % sl])(
                        E.scalar_tensor_tensor(out=h2[:, sl, c, :], in0=xT[:, sl, c, :], scalar=gp[:, G_LN2 + c:G_LN2 + c + 1],
                                               in1=rs, op0=ALU.mult, op1=ALU.mult))
                op('sp', reads=['h2%d' % sl], dma='c1hs%d' % sl)(
                    SP.dma_start(out=h2d[:, 1 + t0:1 + t0 + 512].rearrange("(c p) t -> p c t", p=128), in_=h2[:, sl]))
        sc.barrier()

        with ExitStack() as _st:
            wg_h = _st.enter_context(nc.sbuf_tensor(un("c2_wg"), [128, 8, D_FF], BF16))
            wu_h = _st.enter_context(nc.sbuf_tensor(un("c2_wu"), [128, 8, D_FF], BF16))
            wd_h = _st.enter_context(nc.sbuf_tensor(un("c2_wd"), [128, NJ, D_MODEL], BF16))
            wg, wu, wd = wg_h.ap(), wu_h.ap(), wd_h.ap()
            with ExitStack() as _st:
                stg_h = _st.enter_context(nc.sbuf_tensor(un("c2_stage"), [128, 2, D_FF], F32))
                stg = stg_h.ap()
                for c in range(8):
                    load_w(stg, wg[:, c, :], wg_d[l, c * 128:(c + 1) * 128, :], D_FF, 'wg.%d' % c)
                    load_w(stg, wu[:, c, :], wu_d[l, c * 128:(c + 1) * 128, :], D_FF, 'wu.%d' % c)
                for j in range(NJ):
                    load_w(stg, wd[:, j, :], wd_d[l, j * 128:(j + 1) * 128, :], D_MODEL, 'wd.%d' % j)
                sc.barrier()
            with ExitStack() as _st:
                TW = 510
                h2 = _st.enter_context(nc.sbuf_tensor(un("c2_h2"), [128, 2, 8, TW + 2], BF16)).ap()
                x1 = _st.enter_context(nc.sbuf_tensor(un("c2_x1"), [128, 3, TW], F32)).ap()
                gb = _st.enter_context(nc.sbuf_tensor(un("c2_g"), [128, 2, TW], F32)).ap()
                act = _st.enter_context(nc.sbuf_tensor(un("c2_act"), [128, NJ, TW], BF16)).ap()
                tiles2 = [(t0, min(TW, S - t0)) for t0 in range(0, S, TW)]

                def load_c2(i):
                    sl = i % 2
                    t0, W = tiles2[i]
                    op('sp', writes=['h2%d' % sl], dma='c2h%d' % sl)(
                        SP.dma_start(out=h2[:, sl, :, 0:W + 2], in_=h2d[:, t0:t0 + W + 2].rearrange("(c p) t -> p c t", p=128)))

                xc = [0]

                def load_x1(i, m):
                    t0, W = tiles2[i]
                    xs_ = (i * 8 + m) % 3
                    op('sp', writes=['x1c%d' % xs_], dma='c2x%d' % xs_)(
                        SP.dma_start(out=x1[:, xs_, 0:W], in_=xTb[m * 128:(m + 1) * 128, t0:t0 + W]))

                load_c2(0)
                jc = [0]
                for i in range(len(tiles2)):
                    sl = i % 2
                    t0, W = tiles2[i]
                    if i + 1 < len(tiles2):
                        load_c2(i + 1)
                    hk_ = 'h2%d' % sl
                    for j in range(NJ):
                        js = jc[0] % 2
                        jc[0] += 1
                        bg = nb()
                        for c in range(8):
                            op('pe', reads=[hk_, 'wg'], writes=[pb(bg)])(
                                PE.matmul(ps[:, bg, 0:W + 2], lhsT=wg[:, c, j * 128:(j + 1) * 128], rhs=h2[:, sl, c, 0:W + 2], start=(c == 0), stop=(c == 7)))
                        bu = nb()
                        for c in range(8):
                            op('pe', reads=[hk_, 'wu'], writes=[pb(bu)])(
                                PE.matmul(ps[:, bu, 0:W], lhsT=wu[:, c, j * 128:(j + 1) * 128], rhs=h2[:, sl, c, 1:W + 1], start=(c == 0), stop=(c == 7)))
                        cw = G_CW + 3 * j
                        op('act', reads=[pb(bg), 'gp'], writes=['g%d' % js])(
                            ACT.activation(out=gb[:, js, 0:W], in_=ps[:, bg, 0:W], func=AF.Identity, bias=gp[:, G_CB + j:G_CB + j + 1],
                                           scale=gp[:, cw:cw + 1]))
                        op('dve', reads=[pb(bg), 'g%d' % js, 'gp'], writes=['g%d' % js])(
                            DVE.scalar_tensor_tensor(out=gb[:, js, 0:W], in0=ps[:, bg, 1:W + 1], scalar=gp[:, cw + 1:cw + 2], in1=gb[:, js, 0:W],
                                                     op0=ALU.mult, op1=ALU.add))
                        op('dve', reads=[pb(bg), 'g%d' % js, 'gp'], writes=['g%d' % js])(
                            DVE.scalar_tensor_tensor(out=gb[:, js, 0:W], in0=ps[:, bg, 2:W + 2], scalar=gp[:, cw + 2:cw + 3], in1=gb[:, js, 0:W],
                                                     op0=ALU.mult, op1=ALU.add))
                        op('act', reads=['g%d' % js], writes=['g%d' % js])(ACT.activation(out=gb[:, js, 0:W], in_=gb[:, js, 0:W], func=AF.Silu))
                        op('dve', reads=['g%d' % js, pb(bu)], writes=['act.%d' % j])(
                            DVE.tensor_tensor(out=act[:, j, 0:W], in0=ps[:, bu, 0:W], in1=gb[:, js, 0:W], op=ALU.mult))
                        if j >= NJ - 3:
                            load_x1(i, j - (NJ - 3))
                    for m in range(8):
                        b = nb()
                        for j in range(NJ):
                            op('pe', reads=['act.%d' % j, 'wd'], writes=[pb(b)])(
                                PE.matmul(ps[:, b, 0:W], lhsT=wd[:, j, m * 128:(m + 1) * 128], rhs=act[:, j, 0:W], start=(j == 0), stop=(j == NJ - 1)))
                        xs_ = (i * 8 + m) % 3
                        xk = 'x1c%d' % xs_
                        op('dve', reads=[pb(b), xk], writes=[xk])(
                            DVE.tensor_tensor(out=x1[:, xs_, 0:W], in0=ps[:, b, 0:W], in1=x1[:, xs_, 0:W], op=ALU.add))
                        op('sp', reads=[xk], dma='c2xs%d' % xs_)(
                            SP.dma_start(out=xTa[m * 128:(m + 1) * 128, t0:t0 + W], in_=x1[:, xs_, 0:W]))
                        if m + 3 < 8:
                            load_x1(i, m + 3)
        sc.barrier()

    with ExitStack() as _st:
        xt_h = _st.enter_context(nc.sbuf_tensor(un("z_xt"), [128, 2, 8, 512], F32))
        y_h = _st.enter_context(nc.sbuf_tensor(un("z_y"), [128, 2, 4, D_MODEL], F32))
        xt, yb = xt_h.ap(), y_h.ap()
        for i in range(NT):
            sl = i % 2
            op('sp', writes=['zx%d' % sl], dma='zl%d' % sl)(
                SP.dma_start(out=xt[:, sl], in_=xTa[:, i * 512:(i + 1) * 512].rearrange("(c p) t -> p c t", p=128)))
            for s in range(4):
                for half in range(2):
                    b = nb()
                    for cc in range(4):
                        c = half * 4 + cc
                        op('pe', reads=['zx%d' % sl, 'ident'], writes=[pb(b)])(
                            PE.transpose(ps[:, b, cc * 128:(cc + 1) * 128], xt[:, sl, c, s * 128:(s + 1) * 128], ident))
                    if half == 0:
                        op('dve', reads=[pb(b)], writes=['zy%d.%d.%d' % (sl, s, half)])(
                            DVE.tensor_copy(out=yb[:, sl, s, half * 512:(half + 1) * 512], in_=ps[:, b, :]))
                    else:
                        op('act', reads=[pb(b)], writes=['zy%d.%d.%d' % (sl, s, half)])(
                            ACT.copy(out=yb[:, sl, s, half * 512:(half + 1) * 512], in_=ps[:, b, :]))
            op('sp', reads=['zy%d.%d.%d' % (sl, s, hf) for s in range(4) for hf in range(2)], dma='zs%d' % sl)(
                SP.dma_start(out=y_out[i * 512:(i + 1) * 512, :].rearrange("(s p) f -> p s f", p=128), in_=yb[:, sl]))
    sc.barrier()
    return nc, sc


def _rope_tables(S):
    pos = np.arange(S, dtype=np.float32)
    fm = (np.float32(MLA_THETA) ** (-np.arange(16, dtype=np.float32) * np.float32(2.0) / np.float32(32))).astype(np.float32)
    angm = (pos[:, None] * fm[None, :]).astype(np.float32)
    tabm = np.zeros((96, 2, S), np.float32)
    tabm[:, 0, :] = 1.0
    tabm[64:80, 0, :] = np.cos(angm).T
    tabm[80:96, 0, :] = np.cos(angm).T
    tabm[64:80, 1, :] = np.sin(angm).T
    tabm[80:96, 1, :] = np.sin(angm).T
    fd = (np.float32(ROPE_THETA) ** (-np.arange(8, dtype=np.float32) * np.float32(2.0) / np.float32(16))).astype(np.float32)
    angd = (pos[:, None] * fd[None, :]).astype(np.float32)
    tabd = np.zeros((128, 2, S), np.float32)
    tabd[:, 0, :] = 1.0
    for m in range(2):
        o = 64 * m
        tabd[o:o + 8, 0, :] = np.cos(angd).T
        tabd[o + 8:o + 16, 0, :] = np.cos(angd).T
        tabd[o:o + 8, 1, :] = np.sin(angd).T
        tabd[o + 8:o + 16, 1, :] = np.sin(angd).T
    return tabm, tabd


def _rot_mats():
    r96 = np.zeros((96, 96), np.float32)
    for i in range(16):
        r96[64 + 16 + i, 64 + i] = -1.0
        r96[64 + i, 64 + 16 + i] = 1.0
    r128 = np.zeros((128, 128), np.float32)
    for m in range(2):
        o = 64 * m
        for i in range(8):
            r128[o + 8 + i, o + i] = -1.0
            r128[o + i, o + 8 + i] = 1.0
    e32 = np.zeros((32, 96), np.float32)
    for i in range(32):
        e32[i, 64 + i] = 1.0
    return r96, r128, e32


def _prep_shared(S, depth, p):
    f = lambda a: np.ascontiguousarray(np.asarray(a, dtype=np.float32))
    w_kv = f(p['w_kv_up'])[:depth].reshape(depth, 128, 8, 128)
    wkn = np.zeros((depth, 128, 8, 96), np.float32)
    wkn[:, :, :, 0:64] = w_kv[:, :, :, 0:64]
    wv = np.ascontiguousarray(w_kv[:, :, :, 64:128]).reshape(depth, 128, 512)
    gpack = np.zeros((depth, 128, NG), np.float32)
    for l in range(depth):
        gpack[l, :, G_LN1:G_LN1 + 8] = f(p['ln1_g'])[l].reshape(8, 128).T
        gpack[l, :, G_QN:G_QN + 2] = f(p['mla_q_norm_g'])[l].reshape(2, 128).T
        gpack[l, :, G_KVN] = f(p['mla_kv_norm_g'])[l]
        gpack[l, 0:96, G_MQ] = f(p['mla_qn_g'])[l]
        gpack[l, 0:96, G_MK] = f(p['mla_kn_g'])[l]
        gpack[l, 0:64, G_DQ] = f(p['diff_qn_g'])[l]
        gpack[l, 64:128, G_DQ] = f(p['diff_qn_g'])[l]
        gpack[l, 0:64, G_DK] = f(p['diff_kn_g'])[l]
        gpack[l, 64:128, G_DK] = f(p['diff_kn_g'])[l]
        gpack[l, :, G_SUB] = f(p['diff_subln_g'])[l]
        gpack[l, :, G_LN2:G_LN2 + 8] = f(p['ln2_g'])[l].reshape(8, 128).T
        cw = f(p['conv_w'])[l]
        for j in range(NJ):
            for k in range(3):
                gpack[l, :, G_CW + 3 * j + k] = cw[k, j * 128:(j + 1) * 128]
        gpack[l, :, G_CB:G_CB + NJ] = f(p['conv_b'])[l].reshape(NJ, 128).T
    lamv = np.zeros((depth, 128, 4, 64), np.float32)
    for l in range(depth):
        for a, nm in enumerate(('lambda_q1', 'lambda_k1', 'lambda_q2', 'lambda_k2')):
            lamv[l, :, a, :] = f(p[nm])[l][None, :]
    tabm, tabd = _rope_tables(S)
    r96, r128, e32 = _rot_mats()
    return {
        "w_in": f(p['w_in'])[:depth], "wq": f(p['w_q_up'])[:depth], "wkn": wkn.reshape(depth, 128, 768), "wv": wv,
        "wout": f(p['w_out'])[:depth], "wg": f(p['w_gate'])[:depth], "wu": f(p['w_up'])[:depth], "wd": f(p['w_down'])[:depth],
        "gpack": gpack, "lamv": lamv.reshape(depth, 128, 256), "ident": np.eye(128, dtype=np.float32),
        "r96": r96, "r128": r128, "e32": e32, "tabm": tabm, "tabd": tabd,
    }


def run_trunk(seqs, params, depth=DEPTH):
    S = seqs[0].shape[0]
    lam_inits = [0.8 - 0.6 * math.exp(-0.3 * l) for l in range(depth)]
    nc, sc = build(S, depth, lam_inits)
    shared = _prep_shared(S, depth, params)
    n = len(seqs)
    in_maps = []
    for c in range(8):
        m = dict(shared)
        m["x"] = np.ascontiguousarray(seqs[c % n], dtype=np.float32)
        in_maps.append(m)
    res = run_bass_kernel_spmd(nc, in_maps, core_ids=list(range(8)))
    return [np.asarray(res.results[c]["y"], dtype=np.float32) for c in range(n)]


def kernel(x_prompt, x_sample, ln1_g, w_in, mla_q_norm_g, w_q_up, mla_kv_norm_g, w_kv_up,
           mla_qn_g, mla_kn_g, diff_qn_g, diff_kn_g, lambda_q1, lambda_k1, lambda_q2,
           lambda_k2, diff_subln_g, w_out, ln2_g, w_gate, conv_w, conv_b, w_up, w_down):
    params = dict(ln1_g=ln1_g, w_in=w_in, mla_q_norm_g=mla_q_norm_g, w_q_up=w_q_up,
                  mla_kv_norm_g=mla_kv_norm_g, w_kv_up=w_kv_up, mla_qn_g=mla_qn_g,
                  mla_kn_g=mla_kn_g, diff_qn_g=diff_qn_g, diff_kn_g=diff_kn_g,
                  lambda_q1=lambda_q1, lambda_k1=lambda_k1, lambda_q2=lambda_q2,
                  lambda_k2=lambda_k2, diff_subln_g=diff_subln_g, w_out=w_out, ln2_g=ln2_g,
                  w_gate=w_gate, conv_w=conv_w, conv_b=conv_b, w_up=w_up, w_down=w_down)
    xp = np.asarray(x_prompt, dtype=np.float32)
    xs = np.asarray(x_sample, dtype=np.float32)
    seqs = [xp[b] for b in range(xp.shape[0])] + [xs[b] for b in range(xs.shape[0])]
    outs = run_trunk(seqs, params, DEPTH)
    y_prompt = np.stack(outs[:xp.shape[0]], axis=0)
    y_sample = np.stack(outs[xp.shape[0]:], axis=0)
    return (y_prompt, y_sample)
```

```python
import math
from contextlib import ExitStack
import numpy as np
import concourse.bass as bass
import concourse.mybir as mybir
from concourse.bass_utils import run_bass_kernel_spmd

F32 = mybir.dt.float32
BF16 = mybir.dt.bfloat16
AF = mybir.ActivationFunctionType
ALU = mybir.AluOpType
AX = mybir.AxisListType

D_MODEL = 1024
DEPTH = 4
MLA_HEADS = 8
DIFF_HEADS = 4
D_FF = 2816
NJ = D_FF // 128
IN_DIM = 1952
EPS = 1e-6
MLA_THETA = 10000.0
ROPE_THETA = 500000.0
NG = 112
G_LN1, G_QN, G_KVN, G_MQ, G_MK, G_DQ, G_DK, G_SUB, G_LN2, G_CW, G_CB = 0, 8, 10, 11, 12, 13, 14, 15, 16, 24, 90


class Sched:
    def __init__(self, nc):
        self.nc = nc
        self.engs = {'pe': nc.tensor, 'act': nc.scalar, 'dve': nc.vector, 'pool': nc.gpsimd, 'sp': nc.sync}
        self.csem = {e: nc.alloc_semaphore("c_" + e) for e in ('pe', 'act', 'dve', 'pool')}
        self.ccnt = {e: 0 for e in self.csem}
        self.dsem = {}
        self.waited = {e: {} for e in self.engs}
        self.lastw = {}
        self.reads = {}
        self.nops = 0
        self.nwaits = 0

    def _wait(self, e, tok):
        name, sem, val, src = tok
        if self.waited[e].get(name, 0) >= val:
            return
        self.engs[e].wait_ge(sem, val)
        self.waited[e][name] = val
        self.nwaits += 1

    def op(self, e, reads=(), writes=(), dma=None):
        deps = []
        for r in reads:
            t = self.lastw.get(r)
            if t is not None:
                deps.append((t, 'raw'))
        for w in writes:
            t = self.lastw.get(w)
            if t is not None:
                deps.append((t, 'waw'))
            for t in self.reads.get(w, {}).values():
                deps.append((t, 'war'))
        for t, kind in deps:
            if t[3] == e and dma is None:
                if e == 'pe' or (kind != 'raw' and e != 'pool'):
                    continue
            self._wait(e, t)

        def post(ins):
            self.nops += 1
            if dma is None:
                self.ccnt[e] += 1
                tok = ("c_" + e, self.csem[e], self.ccnt[e], e)
                ins.then_inc(self.csem[e], 1)
            else:
                d = self.dsem.get(dma)
                if d is None:
                    d = [self.nc.alloc_semaphore("d_%d" % len(self.dsem)), 0]
                    self.dsem[dma] = d
                d[1] += 16
                ins.then_inc(d[0], 16)
                tok = ("d_" + str(dma), d[0], d[1], 'dma')
            for r in reads:
                self.reads.setdefault(r, {})[tok[0]] = tok
            for w in writes:
                self.lastw[w] = tok
                self.reads[w] = {}
            return ins
        return post

    def barrier(self):
        for e in self.engs:
            for k, d in self.dsem.items():
                if d[1] > 0:
                    self._wait(e, ("d_" + str(k), d[0], d[1], 'dma'))
            for k in self.csem:
                if self.ccnt[k] > 0 and k != e:
                    self._wait(e, ("c_" + k, self.csem[k], self.ccnt[k], k))
        self.lastw = {}
        self.reads = {}


def build(S, depth, lam_inits):
    nc = bass.Bass("TRN2", target_bir_lowering=False)
    NT = S // 512
    KB = S // 128
    NQ = S // 512
    NT2 = S // 256

    def din(name, shape, dt=F32):
        return nc.dram_tensor(name, list(shape), dt, kind="ExternalInput").ap()

    x_in = din("x", [S, D_MODEL])
    y_out = nc.dram_tensor("y", [S, D_MODEL], F32, kind="ExternalOutput").ap()
    w_in_d = din("w_in", [depth, D_MODEL, IN_DIM])
    wq_d = din("wq", [depth, 256, 768])
    wkn_d = din("wkn", [depth, 128, 8 * 96])
    wv_d = din("wv", [depth, 128, 512])
    wout_d = din("wout", [depth, D_MODEL, D_MODEL])
    wg_d = din("wg", [depth, D_MODEL, D_FF])
    wu_d = din("wu", [depth, D_MODEL, D_FF])
    wd_d = din("wd", [depth, D_FF, D_MODEL])
    gpack_d = din("gpack", [depth, 128, NG])
    lamv_d = din("lamv", [depth, 128, 4 * 64])
    ident_d = din("ident", [128, 128])
    r96_d = din("r96", [96, 96])
    r128_d = din("r128", [128, 128])
    e32_d = din("e32", [32, 96])
    tabm_d = din("tabm", [96, 2, S])
    tabd_d = din("tabd", [128, 2, S])

    xTa = nc.dram_tensor("xTa", [D_MODEL, S], F32).ap()
    xTb = nc.dram_tensor("xTb", [D_MODEL, S], F32).ap()
    qm = nc.dram_tensor("qm", [8, 96, S], BF16).ap()
    km = nc.dram_tensor("km", [8, 96, S], BF16).ap()
    vm = nc.dram_tensor("vm", [S, 512], BF16).ap()
    qd = nc.dram_tensor("qd", [4, 128, S], BF16).ap()
    kd = nc.dram_tensor("kd", [4, 128, S], BF16).ap()
    vd = nc.dram_tensor("vd", [S, 512], BF16).ap()
    oT = nc.dram_tensor("oT", [D_MODEL, S], BF16).ap()
    h2d = nc.dram_tensor("h2d", [D_MODEL, S + 2], BF16).ap()

    sc = Sched(nc)
    uctr = [0]

    def un(name):
        uctr[0] += 1
        return "%s_%d" % (name, uctr[0])

    op = sc.op
    PE, ACT, DVE, POOL, SP = nc.tensor, nc.scalar, nc.vector, nc.gpsimd, nc.sync

    ps = nc.alloc_psum_tensor("ps", [128, 8, 512], F32).ap()

    ident = nc.alloc_sbuf_tensor("s_ident", [128, 128], F32).ap()
    ones_b = nc.alloc_sbuf_tensor("s_ones_b", [128, 128], BF16).ap()
    bd64_b = nc.alloc_sbuf_tensor("s_bd64_b", [128, 128], BF16).ap()
    r96_b = nc.alloc_sbuf_tensor("s_r96_b", [96, 96], BF16).ap()
    r128_b = nc.alloc_sbuf_tensor("s_r128_b", [128, 128], BF16).ap()
    e32_b = nc.alloc_sbuf_tensor("s_e32_b", [32, 96], BF16).ap()
    gp = nc.alloc_sbuf_tensor("s_gp", [128, NG], F32).ap()
    nlam = nc.alloc_sbuf_tensor("s_nlam", [128, 1], F32).ap()
    cst = nc.alloc_sbuf_tensor("s_cst", [128, 128], F32).ap()
    epsb = nc.alloc_sbuf_tensor("s_epsb", [128, 1], F32).ap()
    zcol = nc.alloc_sbuf_tensor("s_zcol", [128, 8], BF16).ap()

    op('sp', writes=['ident'], dma='c0')(SP.dma_start(out=ident, in_=ident_d))
    op('pool', writes=['ones_b'])(POOL.memset(ones_b, 1.0))
    op('pool', writes=['epsb'])(POOL.memset(epsb, EPS))
    op('pool', writes=['zcol'])(POOL.memset(zcol, 0.0))
    op('pool', writes=['bd64_b'])(POOL.memset(bd64_b, 0.0))
    op('pool', writes=['bd64_b'])(POOL.memset(bd64_b[0:64, 0:64], 1.0))
    op('pool', writes=['bd64_b'])(POOL.memset(bd64_b[64:128, 64:128], 1.0))
    op('sp', writes=['cst'], dma='c1')(SP.dma_start(out=cst[0:96, 0:96], in_=r96_d))
    op('dve', reads=['cst'], writes=['r96_b'])(DVE.tensor_copy(out=r96_b, in_=cst[0:96, 0:96]))
    op('sp', reads=[], writes=['cst'], dma='c1')(SP.dma_start(out=cst, in_=r128_d))
    op('dve', reads=['cst'], writes=['r128_b'])(DVE.tensor_copy(out=r128_b, in_=cst))
    op('sp', writes=['cst'], dma='c1')(SP.dma_start(out=cst[0:32, 0:96], in_=e32_d))
    op('dve', reads=['cst'], writes=['e32_b'])(DVE.tensor_copy(out=e32_b, in_=cst[0:32, 0:96]))
    for c in range(8):
        op('sp', reads=['zcol'], dma='c2')(SP.dma_start(out=h2d[c * 128:(c + 1) * 128, 0:1], in_=zcol[:, 0:1], allow_slow_non_contiguous=True))
        op('sp', reads=['zcol'], dma='c3')(SP.dma_start(out=h2d[c * 128:(c + 1) * 128, S + 1:S + 2], in_=zcol[:, 1:2], allow_slow_non_contiguous=True))
    sc.barrier()

    bank_ctr = [0]

    def nb():
        b = bank_ctr[0] % 8
        bank_ctr[0] += 1
        return b

    def pb(b):
        return 'ps%d' % b

    def load_w(stage, dst, src, ncols, key, scale=None, eng_i=[0]):
        P = dst.shape[0]
        sl = eng_i[0] % 2
        eng_i[0] += 1
        st = stage[0:P, sl, 0:ncols]
        op('sp', writes=['stage%d' % sl], dma='wst%d' % sl)(SP.dma_start(out=st, in_=src))
        e = ('pool', 'dve')[sl]
        E = (POOL, DVE)[sl]
        if scale is None:
            op(e, reads=['stage%d' % sl], writes=[key])(E.tensor_copy(out=dst, in_=st))
        else:
            op(e, reads=['stage%d' % sl], writes=[key])(E.tensor_scalar(out=dst, in0=st, scalar1=float(scale), scalar2=None, op0=ALU.mult))

    with ExitStack() as _st:
        xin_h = _st.enter_context(nc.sbuf_tensor(un("p0_xin"), [128, 2, 4, D_MODEL], F32))
        xt_h = _st.enter_context(nc.sbuf_tensor(un("p0_xt"), [128, 2, 8, 512], F32))
        xin = xin_h.ap()
        xt = xt_h.ap()
        for i in range(NT):
            sl = i % 2
            op('sp', writes=['xin%d' % sl], dma='p0l%d' % sl)(
                SP.dma_start(out=xin[:, sl], in_=x_in[i * 512:(i + 1) * 512, :].rearrange("(s p) f -> p s f", p=128)))
            for c in range(8):
                b = nb()
                for s in range(4):
                    op('pe', reads=['xin%d' % sl, 'ident'], writes=[pb(b)])(
                        PE.transpose(ps[:, b, s * 128:(s + 1) * 128], xin[:, sl, s, c * 128:(c + 1) * 128], ident))
                if c % 2 == 0:
                    op('dve', reads=[pb(b)], writes=['xt%d.%d' % (sl, c)])(DVE.tensor_copy(out=xt[:, sl, c, :], in_=ps[:, b, :]))
                else:
                    op('act', reads=[pb(b)], writes=['xt%d.%d' % (sl, c)])(ACT.copy(out=xt[:, sl, c, :], in_=ps[:, b, :]))
            op('sp', reads=['xt%d.%d' % (sl, c) for c in range(8)], dma='p0s%d' % sl)(
                SP.dma_start(out=xTa[:, i * 512:(i + 1) * 512].rearrange("(c p) t -> p c t", p=128), in_=xt[:, sl]))
    sc.barrier()

    for l in range(depth):
        lam_init = lam_inits[l]
        with ExitStack() as _st:
            lamv_h = _st.enter_context(nc.sbuf_tensor(un("s_lamv"), [128, 4, 64], F32))
            lamt_h = _st.enter_context(nc.sbuf_tensor(un("s_lamt"), [128, 8], F32))
            lamv = lamv_h.ap()
            lamt = lamt_h.ap()
            op('sp', writes=['gp'], dma='c0')(SP.dma_start(out=gp, in_=gpack_d[l]))
            op('sp', writes=['lamv'], dma='c1')(SP.dma_start(out=lamv, in_=lamv_d[l].rearrange("p (a b) -> p a b", a=4)))
            op('dve', reads=['lamv'], writes=['lamv'])(DVE.tensor_tensor(out=lamv[:, 0, :], in0=lamv[:, 0, :], in1=lamv[:, 1, :], op=ALU.mult))
            op('dve', reads=['lamv'], writes=['lamv'])(DVE.tensor_tensor(out=lamv[:, 2, :], in0=lamv[:, 2, :], in1=lamv[:, 3, :], op=ALU.mult))
            op('dve', reads=['lamv'], writes=['lamt'])(DVE.reduce_sum(out=lamt[:, 0:1], in_=lamv[:, 0, :], axis=AX.X))
            op('dve', reads=['lamv', 'lamt'], writes=['lamt'])(DVE.reduce_sum(out=lamt[:, 1:2], in_=lamv[:, 2, :], axis=AX.X))
            op('act', reads=['lamt'], writes=['lamt2'])(ACT.activation(out=lamt[:, 2:4], in_=lamt[:, 0:2], func=AF.Exp))
            op('dve', reads=['lamt2'], writes=['lamt3'])(DVE.tensor_tensor(out=lamt[:, 4:5], in0=lamt[:, 3:4], in1=lamt[:, 2:3], op=ALU.subtract))
            op('dve', reads=['lamt3'], writes=['nlam'])(DVE.tensor_scalar(out=nlam, in0=lamt[:, 4:5], scalar1=-float(lam_init), scalar2=None, op0=ALU.add))
            sc.barrier()

        with ExitStack() as _st:
            win_h = _st.enter_context(nc.sbuf_tensor(un("a_win"), [128, 8, IN_DIM], BF16))
            wq_h = _st.enter_context(nc.sbuf_tensor(un("a_wq"), [128, 2, 768], BF16))
            wkn_h = _st.enter_context(nc.sbuf_tensor(un("a_wkn"), [128, 8, 96], BF16))
            wv_h = _st.enter_context(nc.sbuf_tensor(un("a_wv"), [128, 512], BF16))
            stg_h = _st.enter_context(nc.sbuf_tensor(un("a_stage"), [128, 2, IN_DIM], F32))
            xT_h = _st.enter_context(nc.sbuf_tensor(un("a_xT"), [128, 2, 8, 512], F32))
            tm_h = _st.enter_context(nc.sbuf_tensor(un("a_tm"), [96, 2, 2, 512], F32))
            td_h = _st.enter_context(nc.sbuf_tensor(un("a_td"), [128, 2, 2, 512], F32))
            sq_h = _st.enter_context(nc.sbuf_tensor(un("a_sq"), [128, 8, 512], BF16))
            hT_h = _st.enter_context(nc.sbuf_tensor(un("a_hT"), [128, 8, 512], BF16))
            rs_h = _st.enter_context(nc.sbuf_tensor(un("a_rs"), [128, 2, 512], F32))
            cqn_h = _st.enter_context(nc.sbuf_tensor(un("a_cqn"), [128, 2, 512], BF16))
            ckvn_h = _st.enter_context(nc.sbuf_tensor(un("a_ckvn"), [128, 512], BF16))
            kpe_h = _st.enter_context(nc.sbuf_tensor(un("a_kpe"), [32, 512], BF16))
            hsq_h = _st.enter_context(nc.sbuf_tensor(un("a_hsq"), [128, 4, 512], BF16))
            hrs_h = _st.enter_context(nc.sbuf_tensor(un("a_hrs"), [128, 3, 512], F32))
            hzn_h = _st.enter_context(nc.sbuf_tensor(un("a_hzn"), [128, 4, 512], BF16))
            ht1_h = _st.enter_context(nc.sbuf_tensor(un("a_ht1"), [128, 4, 512], F32))
            ht2_h = _st.enter_context(nc.sbuf_tensor(un("a_ht2"), [128, 2, 512], F32))
            hzf_h = _st.enter_context(nc.sbuf_tensor(un("a_hzf"), [128, 3, 512], BF16))
            vst_h = _st.enter_context(nc.sbuf_tensor(un("a_vst"), [128, 2, 4, 512], BF16))
            win, wq, wkn, wv, stg = win_h.ap(), wq_h.ap(), wkn_h.ap(), wv_h.ap(), stg_h.ap()
            xT, tm, td, sq, hT, rs = xT_h.ap(), tm_h.ap(), td_h.ap(), sq_h.ap(), hT_h.ap(), rs_h.ap()
            cqn, ckvn, kpe = cqn_h.ap(), ckvn_h.ap(), kpe_h.ap()
            hsq, hrs, hzn, ht1, ht2, hzf, vst = hsq_h.ap(), hrs_h.ap(), hzn_h.ap(), ht1_h.ap(), ht2_h.ap(), hzf_h.ap(), vst_h.ap()

            for c in range(8):
                load_w(stg, win[:, c, :], w_in_d[l, c * 128:(c + 1) * 128, :], IN_DIM, 'win')
            for c in range(2):
                load_w(stg, wq[:, c, :], wq_d[l, c * 128:(c + 1) * 128, :], 768, 'wq')
            load_w(stg, wkn.rearrange("p a b -> p (a b)"), wkn_d[l], 768, 'wkn')
            load_w(stg, wv, wv_d[l], 512, 'wv')

            hk = [0]

            def head_stage1(ht):
                ht['k'] = hk[0]
                hk[0] += 1
                k = ht['k']
                D = ht['D']
                b = k % 4
                ht['zb'] = b
                ht['zemit'](b)
                op('act', reads=[pb(b)], writes=['hsq%d' % (k % 4)])(ACT.activation(out=hsq[0:D, k % 4, :], in_=ps[0:D, b, :], func=AF.Square))

            def head_stage2a(ht):
                k, D = ht['k'], ht['D']
                b = 4 + k % 2
                op('pe', reads=['hsq%d' % (k % 4), ht['nmk']], writes=[pb(b)])(
                    PE.matmul(ps[0:D, b, :], lhsT=ht['nm'], rhs=hsq[0:D, k % 4, :], start=True, stop=True))
                op('act', reads=[pb(b), 'epsb'], writes=['hrs%d' % (k % 3)])(
                    ACT.activation(out=hrs[0:D, k % 3, :], in_=ps[0:D, b, :], func=AF.Ln, bias=epsb[0:D, :], scale=1.0 / ht['n']))
                op('act', reads=['hrs%d' % (k % 3)], writes=['hrs%d' % (k % 3)])(
                    ACT.activation(out=hrs[0:D, k % 3, :], in_=hrs[0:D, k % 3, :], func=AF.Exp, scale=-0.5))

            def head_stage2b(ht):
                k, D, zb = ht['k'], ht['D'], ht['zb']
                op('dve', reads=[pb(zb), 'hrs%d' % (k % 3), 'gp'], writes=['hzn%d' % (k % 4)])(
                    DVE.scalar_tensor_tensor(out=hzn[0:D, k % 4, :], in0=ps[0:D, zb, :], scalar=ht['g'], in1=hrs[0:D, k % 3, :],
                                             op0=ALU.mult, op1=ALU.mult))
                op('pool', reads=['hzn%d' % (k % 4), ht['tabk']], writes=['ht1%d' % (k % 4)])(
                    POOL.tensor_tensor(out=ht1[0:D, k % 4, :], in0=hzn[0:D, k % 4, :], in1=ht['C'], op=ALU.mult))

            def head_stage3(ht):
                k, D = ht['k'], ht['D']
                b = 6 + k % 2
                op('pe', reads=['hzn%d' % (k % 4), ht['rk']], writes=[pb(b)])(
                    PE.matmul(ps[0:D, b, :], lhsT=ht['R'], rhs=hzn[0:D, k % 4, :], start=True, stop=True))
                op('dve', reads=[pb(b), ht['tabk']], writes=['ht2%d' % (k % 2)])(
                    DVE.tensor_tensor(out=ht2[0:D, k % 2, :], in0=ps[0:D, b, :], in1=ht['S'], op=ALU.mult))
                if k % 2 == 0:
                    op('dve', reads=['ht1%d' % (k % 4), 'ht2%d' % (k % 2)], writes=['hzf%d' % (k % 3)])(
                        DVE.tensor_tensor(out=hzf[0:D, k % 3, :], in0=ht1[0:D, k % 4, :], in1=ht2[0:D, k % 2, :], op=ALU.add))
                else:
                    op('pool', reads=['ht1%d' % (k % 4), 'ht2%d' % (k % 2)], writes=['hzf%d' % (k % 3)])(
                        POOL.tensor_tensor(out=hzf[0:D, k % 3, :], in0=ht1[0:D, k % 4, :], in1=ht2[0:D, k % 2, :], op=ALU.add))
                op('sp', reads=['hzf%d' % (k % 3)], dma='hst%d' % (k % 3))(SP.dma_start(out=ht['dst'], in_=hzf[0:D, k % 3, :]))

            def load_tile(i):
                sl = i % 2
                t0 = i * 512
                op('sp', writes=['xT%d' % sl], dma='axl%d' % sl)(
                    SP.dma_start(out=xT[:, sl], in_=xTa[:, t0:t0 + 512].rearrange("(c p) t -> p c t", p=128)))
                op('sp', writes=['tab%d' % sl], dma='atl%d' % sl)(SP.dma_start(out=tm[:, sl], in_=tabm_d[:, :, t0:t0 + 512]))
                op('sp', writes=['tab%d' % sl], dma='atl%d' % sl)(SP.dma_start(out=td[:, sl], in_=tabd_d[:, :, t0:t0 + 512]))

            load_tile(0)
            for i in range(NT):
                sl = i % 2
                t0 = i * 512
                if i + 1 < NT:
                    load_tile(i + 1)
                xs = 'xT%d' % sl
                tabk = 'tab%d' % sl
                for c in range(8):
                    op('act', reads=[xs], writes=['sq.%d' % c])(ACT.activation(out=sq[:, c, :], in_=xT[:, sl, c, :], func=AF.Square))
                b = nb()
                for c in range(8):
                    op('pe', reads=['sq.%d' % c, 'ones_b'], writes=[pb(b)])(
                        PE.matmul(ps[:, b, :], lhsT=ones_b, rhs=sq[:, c, :], start=(c == 0), stop=(c == 7)))
                op('act', reads=[pb(b), 'epsb'], writes=['rs0'])(
                    ACT.activation(out=rs[:, 0, :], in_=ps[:, b, :], func=AF.Ln, bias=epsb, scale=1.0 / D_MODEL))
                op('act', reads=['rs0'], writes=['rs0'])(ACT.activation(out=rs[:, 0, :], in_=rs[:, 0, :], func=AF.Exp, scale=-0.5))
                for c in range(8):
                    e, E = ('dve', DVE)
                    op(e, reads=[xs, 'rs0', 'gp'], writes=['hT.%d' % c])(
                        E.scalar_tensor_tensor(out=hT[:, c, :], in0=xT[:, sl, c, :], scalar=gp[:, G_LN1 + c:G_LN1 + c + 1],
                                               in1=rs[:, 0, :], op0=ALU.mult, op1=ALU.mult))

                def proj(b, c0, M):
                    for c in range(8):
                        op('pe', reads=['hT.%d' % c, 'win'], writes=[pb(b)])(
                            PE.matmul(ps[0:M, b, :], lhsT=win[:, c, c0:c0 + M], rhs=hT[:, c, :], start=(c == 0), stop=(c == 7)))

                bq = [nb(), nb()]
                for j in range(2):
                    proj(bq[j], 128 * j, 128)
                    op('act', reads=[pb(bq[j])], writes=['sq.%d' % j])(ACT.activation(out=sq[:, j, :], in_=ps[:, bq[j], :], func=AF.Square))
                b = nb()
                for j in range(2):
                    op('pe', reads=['sq.%d' % j, 'ones_b'], writes=[pb(b)])(
                        PE.matmul(ps[:, b, :], lhsT=ones_b, rhs=sq[:, j, :], start=(j == 0), stop=(j == 1)))
                op('act', reads=[pb(b), 'epsb'], writes=['rs1'])(
                    ACT.activation(out=rs[:, 1, :], in_=ps[:, b, :], func=AF.Ln, bias=epsb, scale=1.0 / 256))
                op('act', reads=['rs1'], writes=['rs1'])(ACT.activation(out=rs[:, 1, :], in_=rs[:, 1, :], func=AF.Exp, scale=-0.5))
                for j in range(2):
                    op('dve', reads=[pb(bq[j]), 'rs1', 'gp'], writes=['cqn.%d' % j])(
                        DVE.scalar_tensor_tensor(out=cqn[:, j, :], in0=ps[:, bq[j], :], scalar=gp[:, G_QN + j:G_QN + j + 1],
                                                 in1=rs[:, 1, :], op0=ALU.mult, op1=ALU.mult))
                bkv = nb()
                proj(bkv, 256, 128)
                op('act', reads=[pb(bkv)], writes=['sq.2'])(ACT.activation(out=sq[:, 2, :], in_=ps[:, bkv, :], func=AF.Square))
                b = nb()
                op('pe', reads=['sq.2', 'ones_b'], writes=[pb(b)])(PE.matmul(ps[:, b, :], lhsT=ones_b, rhs=sq[:, 2, :], start=True, stop=True))
                op('act', reads=[pb(b), 'epsb'], writes=['rs0'])(
                    ACT.activation(out=rs[:, 0, :], in_=ps[:, b, :], func=AF.Ln, bias=epsb, scale=1.0 / 128))
                op('act', reads=['rs0'], writes=['rs0'])(ACT.activation(out=rs[:, 0, :], in_=rs[:, 0, :], func=AF.Exp, scale=-0.5))
                op('dve', reads=[pb(bkv), 'rs0', 'gp'], writes=['ckvn'])(
                    DVE.scalar_tensor_tensor(out=ckvn, in0=ps[:, bkv, :], scalar=gp[:, G_KVN:G_KVN + 1], in1=rs[:, 0, :],
                                             op0=ALU.mult, op1=ALU.mult))
                b = nb()
                proj(b, 384, 32)
                op('act', reads=[pb(b)], writes=['kpe'])(ACT.copy(out=kpe, in_=ps[0:32, b, :]))

                hts = []
                for h in range(8):
                    def zq(b, h=h):
                        for j in range(2):
                            op('pe', reads=['cqn.%d' % j, 'wq'], writes=[pb(b)])(
                                PE.matmul(ps[0:96, b, :], lhsT=wq[:, j, 96 * h:96 * h + 96], rhs=cqn[:, j, :], start=(j == 0), stop=(j == 1)))
                    hts.append(dict(D=96, n=96, zemit=zq, nm=ones_b[0:96, 0:96], nmk='ones_b', g=gp[0:96, G_MQ:G_MQ + 1],
                                    R=r96_b, rk='r96_b', C=tm[:, sl, 0, :], S=tm[:, sl, 1, :], tabk=tabk, dst=qm[h, :, t0:t0 + 512]))
                for h in range(8):
                    def zk(b, h=h):
                        op('pe', reads=['ckvn', 'wkn'], writes=[pb(b)])(
                            PE.matmul(ps[0:96, b, :], lhsT=wkn[:, h, :], rhs=ckvn, start=True, stop=False))
                        op('pe', reads=['kpe', 'e32_b'], writes=[pb(b)])(
                            PE.matmul(ps[0:96, b, :], lhsT=e32_b, rhs=kpe, start=False, stop=True))
                    hts.append(dict(D=96, n=96, zemit=zk, nm=ones_b[0:96, 0:96], nmk='ones_b', g=gp[0:96, G_MK:G_MK + 1],
                                    R=r96_b, rk='r96_b', C=tm[:, sl, 0, :], S=tm[:, sl, 1, :], tabk=tabk, dst=km[h, :, t0:t0 + 512]))
                for hh in range(4):
                    def zdq(b, hh=hh):
                        proj(b, 416 + 128 * hh, 128)
                    hts.append(dict(D=128, n=64, zemit=zdq, nm=bd64_b, nmk='bd64_b', g=gp[:, G_DQ:G_DQ + 1],
                                    R=r128_b, rk='r128_b', C=td[:, sl, 0, :], S=td[:, sl, 1, :], tabk=tabk, dst=qd[hh, :, t0:t0 + 512]))
                for hh in range(4):
                    def zdk(b, hh=hh):
                        proj(b, 928 + 128 * hh, 128)
                    hts.append(dict(D=128, n=64, zemit=zdk, nm=bd64_b, nmk='bd64_b', g=gp[:, G_DK:G_DK + 1],
                                    R=r128_b, rk='r128_b', C=td[:, sl, 0, :], S=td[:, sl, 1, :], tabk=tabk, dst=kd[hh, :, t0:t0 + 512]))
                n = len(hts)
                for k in range(n + 5):
                    if 0 <= k - 5 < n:
                        head_stage3(hts[k - 5])
                    if 0 <= k - 3 < n:
                        head_stage2b(hts[k - 3])
                    if 0 <= k - 2 < n:
                        head_stage2a(hts[k - 2])
                    if k < n:
                        head_stage1(hts[k])

                vs = 'vst'
                for s in range(4):
                    b = nb()
                    op('pe', reads=['ckvn', 'wv'], writes=[pb(b)])(
                        PE.matmul(ps[:, b, :], lhsT=ckvn[:, s * 128:(s + 1) * 128], rhs=wv, start=True, stop=True))
                    op('act', reads=[pb(b)], writes=[vs + 'm'])(ACT.copy(out=vst[:, 0, s, :], in_=ps[:, b, :]))
                op('sp', reads=[vs + 'm'], dma='avm')(
                    SP.dma_start(out=vm[t0:t0 + 512, :].rearrange("(s p) f -> p s f", p=128), in_=vst[:, 0]))
                for s in range(4):
                    b = nb()
                    for c in range(8):
                        op('pe', reads=['hT.%d' % c, 'win'], writes=[pb(b)])(
                            PE.matmul(ps[:, b, :], lhsT=hT[:, c, s * 128:(s + 1) * 128], rhs=win[:, c, 1440:1952], start=(c == 0), stop=(c == 7)))
                    op('dve', reads=[pb(b)], writes=[vs + 'd'])(DVE.tensor_copy(out=vst[:, 1, s, :], in_=ps[:, b, :]))
                op('sp', reads=[vs + 'd'], dma='avd')(
                    SP.dma_start(out=vd[t0:t0 + 512, :].rearrange("(s p) f -> p s f", p=128), in_=vst[:, 1]))
        sc.barrier()

        with ExitStack() as _st:
            bq_h = _st.enter_context(nc.sbuf_tensor(un("b_q"), [128, 2, S], BF16))
            bk_h = _st.enter_context(nc.sbuf_tensor(un("b_k"), [128, 2, S], BF16))
            bv_h = _st.enter_context(nc.sbuf_tensor(un("b_v"), [128, 2, KB, 128], BF16))
            bp_h = _st.enter_context(nc.sbuf_tensor(un("b_p"), [128, 3, 3, 512], BF16))
            be_h = _st.enter_context(nc.sbuf_tensor(un("b_e"), [128, 6, 512], F32))
            bsq_h = _st.enter_context(nc.sbuf_tensor(un("b_sq"), [128, 512], BF16))
            bo_h = _st.enter_context(nc.sbuf_tensor(un("b_o"), [128, 2, 512], BF16))
            qb_, kb_, vb_, pbuf, eb, bsq, ob = bq_h.ap(), bk_h.ap(), bv_h.ap(), bp_h.ap(), be_h.ap(), bsq_h.ap(), bo_h.ap()

            heads = [('m', h) for h in range(8)] + [('d', hh) for hh in range(4)]

            def load_head(idx):
                kind, h = heads[idx]
                sl = idx % 2
                if kind == 'm':
                    if idx < 2:
                        op('pool', writes=['V%d' % sl])(POOL.memset(vb_[:, sl, :, 64:128], 1.0))
                    op('sp', writes=['Q%d' % sl], dma='bql%d' % sl)(SP.dma_start(out=qb_[0:96, sl, :], in_=qm[h]))
                    op('sp', writes=['K%d' % sl], dma='bkl%d' % sl)(SP.dma_start(out=kb_[0:96, sl, :], in_=km[h]))
                    for k0 in range(0, KB, 8):
                        op('sp', writes=['V%d' % sl], dma='bvl%d' % sl)(
                            SP.dma_start(out=vb_[:, sl, k0:k0 + 8, 0:64],
                                         in_=vm[k0 * 128:(k0 + 8) * 128, h * 64:(h + 1) * 64].rearrange("(kb p) d -> p kb d", p=128)))
                else:
                    op('sp', writes=['Q%d' % sl], dma='bql%d' % sl)(SP.dma_start(out=qb_[:, sl, :], in_=qd[h]))
                    op('sp', writes=['K%d' % sl], dma='bkl%d' % sl)(SP.dma_start(out=kb_[:, sl, :], in_=kd[h]))
                    for k0 in range(0, KB, 8):
                        op('sp', writes=['V%d' % sl], dma='bvl%d' % sl)(
                            SP.dma_start(out=vb_[:, sl, k0:k0 + 8, :],
                                         in_=vd[k0 * 128:(k0 + 8) * 128, h * 128:(h + 1) * 128].rearrange("(kb p) d -> p kb d", p=128)))

            load_head(0)
            gctr = [0]
            for idx, (kind, h) in enumerate(heads):
                sl = idx % 2
                if idx + 1 < len(heads):
                    load_head(idx + 1)
                Qk, Kk, Vk = 'Q%d' % sl, 'K%d' % sl, 'V%d' % sl
                if kind == 'm':
                    G = 3
                    tiles = [(0, kb) for kb in range(KB)]
                    rows = [(0, 96)]
                    scale = 96 ** -0.5
                    sbase = [0, 3]
                else:
                    G = 2
                    tiles = [(m, kb) for kb in range(KB) for m in (0, 1)]
                    rows = [(0, 64), (64, 128)]
                    scale = 64 ** -0.5
                    sbase = [0, 2]
                groups = [tiles[a:a + G] for a in range(0, len(tiles), G)]
                for qi in range(NQ):
                    q0 = qi * 512
                    if kind == 'm':
                        accb = [6 + (qi % 2)]
                        sumb = None
                    else:
                        accb = [4, 5]
                        sumb = [6, 7]

                    def emit_qk(gi):
                        sb = sbase[gi % 2]
                        for j, (m, kb) in enumerate(groups[gi]):
                            r0, r1 = rows[m]
                            op('pe', reads=[Qk, Kk], writes=[pb(sb + j)])(
                                PE.matmul(ps[:, sb + j, :], lhsT=kb_[r0:r1, sl, kb * 128:(kb + 1) * 128], rhs=qb_[r0:r1, sl, q0:q0 + 512],
                                          start=True, stop=True))

                    def emit_exp(gi, pslot):
                        sb = sbase[gi % 2]
                        ng = len(groups[gi])
                        op('act', reads=[pb(sb + j) for j in range(ng)], writes=['P%d' % pslot])(
                            ACT.activation(out=pbuf[:, pslot, 0:ng, :], in_=ps[:, sb:sb + ng, :], func=AF.Exp, scale=float(scale)))

                    def emit_pv(gi, pslot):
                        for j, (m, kb) in enumerate(groups[gi]):
                            first = (kb == 0)
                            last = (kb == KB - 1)
                            op('pe', reads=['P%d' % pslot, Vk], writes=[pb(accb[m])])(
                                PE.matmul(ps[:, accb[m], :], lhsT=vb_[:, sl, kb, :], rhs=pbuf[:, pslot, j, :], start=first, stop=last))
                            if sumb is not None:
                                op('pe', reads=['P%d' % pslot, 'ones_b'], writes=[pb(sumb[m])])(
                                    PE.matmul(ps[:, sumb[m], :], lhsT=ones_b, rhs=pbuf[:, pslot, j, :], start=first, stop=last))

                    ng_ = len(groups)
                    emit_qk(0)
                    if ng_ > 1:
                        emit_qk(1)
                    for gi in range(ng_):
                        pslot = gctr[0] % 3
                        gctr[0] += 1
                        emit_exp(gi, pslot)
                        if gi + 2 < ng_:
                            emit_qk(gi + 2)
                        emit_pv(gi, pslot)

                    osl = qi % 2
                    if kind == 'm':
                        a = accb[0]
                        op('dve', reads=[pb(a)], writes=['e0'])(DVE.reciprocal(out=eb[0:64, 0, :], in_=ps[64:128, a, :]))
                        op('dve', reads=[pb(a), 'e0'], writes=['o%d' % osl])(
                            DVE.tensor_tensor(out=ob[0:64, osl, :], in0=ps[0:64, a, :], in1=eb[0:64, 0, :], op=ALU.mult))
                        op('sp', reads=['o%d' % osl], dma='bos%d' % osl)(
                            SP.dma_start(out=oT[h * 64:(h + 1) * 64, q0:q0 + 512], in_=ob[0:64, osl, :]))
                    else:
                        op('dve', reads=[pb(6)], writes=['e0'])(DVE.tensor_copy(out=eb[:, 0, :], in_=ps[:, 6, :]))
                        op('dve', reads=[pb(7)], writes=['e1'])(DVE.tensor_copy(out=eb[:, 1, :], in_=ps[:, 7, :]))
                        op('dve', reads=[pb(4)], writes=['e2'])(DVE.tensor_copy(out=eb[:, 2, :], in_=ps[:, 4, :]))
                        op('dve', reads=[pb(5)], writes=['e3'])(DVE.tensor_copy(out=eb[:, 3, :], in_=ps[:, 5, :]))
                        op('dve', reads=['e0'], writes=['e0'])(DVE.reciprocal(out=eb[:, 0, :], in_=eb[:, 0, :]))
                        op('dve', reads=['e1'], writes=['e1'])(DVE.reciprocal(out=eb[:, 1, :], in_=eb[:, 1, :]))
                        op('dve', reads=['e2', 'e0'], writes=['e2'])(DVE.tensor_tensor(out=eb[:, 2, :], in0=eb[:, 2, :], in1=eb[:, 0, :], op=ALU.mult))
                        op('dve', reads=['e3', 'e1'], writes=['e3'])(DVE.tensor_tensor(out=eb[:, 3, :], in0=eb[:, 3, :], in1=eb[:, 1, :], op=ALU.mult))
                        op('dve', reads=['e2', 'e3', 'nlam'], writes=['o%d' % osl])(
                            DVE.scalar_tensor_tensor(out=ob[:, osl, :], in0=eb[:, 3, :], scalar=nlam[:, 0:1], in1=eb[:, 2, :], op0=ALU.mult, op1=ALU.add))
                        op('sp', reads=['o%d' % osl], dma='bos%d' % osl)(
                            SP.dma_start(out=oT[512 + h * 128:512 + (h + 1) * 128, q0:q0 + 512], in_=ob[:, osl, :]))
        sc.barrier()

        with ExitStack() as _st:
            wout_h = _st.enter_context(nc.sbuf_tensor(un("c1_wout"), [128, 8, D_MODEL], BF16))
            stg_h = _st.enter_context(nc.sbuf_tensor(un("c1_stage"), [128, 2, D_MODEL], F32))
            o_h = _st.enter_context(nc.sbuf_tensor(un("c1_oT"), [128, 2, 8, 512], BF16))
            xT_h = _st.enter_context(nc.sbuf_tensor(un("c1_xT"), [128, 2, 8, 512], F32))
            sq_h = _st.enter_context(nc.sbuf_tensor(un("c1_sq"), [128, 8, 512], BF16))
            h2_h = _st.enter_context(nc.sbuf_tensor(un("c1_h2"), [128, 2, 8, 512], BF16))
            rs_h = _st.enter_context(nc.sbuf_tensor(un("c1_rs"), [128, 512], F32))
            sqd = _st.enter_context(nc.sbuf_tensor(un("c1_sqd"), [128, 4, 512], BF16)).ap()
            rsd = _st.enter_context(nc.sbuf_tensor(un("c1_rsd"), [128, 4, 512], F32)).ap()
            wout, stg, ot, xT, sq, h2, rs = wout_h.ap(), stg_h.ap(), o_h.ap(), xT_h.ap(), sq_h.ap(), h2_h.ap(), rs_h.ap()
            for c in range(8):
                load_w(stg, wout[:, c, :], wout_d[l, c * 128:(c + 1) * 128, :], D_MODEL, 'wout',
                       scale=(None if c < 4 else (1.0 - lam_init)))

            def load_c1(i):
                sl = i % 2
                t0 = i * 512
                op('sp', writes=['ot%d.%d' % (sl, c) for c in range(8)], dma='c1o%d' % sl)(
                    SP.dma_start(out=ot[:, sl], in_=oT[:, t0:t0 + 512].rearrange("(c p) t -> p c t", p=128)))
                op('sp', writes=['x%d.%d' % (sl, c) for c in range(8)], dma='c1x%d' % sl)(
                    SP.dma_start(out=xT[:, sl], in_=xTa[:, t0:t0 + 512].rearrange("(c p) t -> p c t", p=128)))

            load_c1(0)
            for i in range(NT):
                sl = i % 2
                t0 = i * 512
                if i + 1 < NT:
                    load_c1(i + 1)
                for hh in range(4):
                    c = 4 + hh
                    ok = 'ot%d.%d' % (sl, c)
                    op('act', reads=[ok], writes=['sqd.%d' % hh])(ACT.activation(out=sqd[:, hh, :], in_=ot[:, sl, c, :], func=AF.Square))
                    b = nb()
                    op('pe', reads=['sqd.%d' % hh, 'ones_b'], writes=[pb(b)])(PE.matmul(ps[:, b, :], lhsT=ones_b, rhs=sqd[:, hh, :], start=True, stop=True))
                    op('act', reads=[pb(b), 'epsb'], writes=['rsd.%d' % hh])(
                        ACT.activation(out=rsd[:, hh, :], in_=ps[:, b, :], func=AF.Ln, bias=epsb, scale=1.0 / 128))
                    op('act', reads=['rsd.%d' % hh], writes=['rsd.%d' % hh])(ACT.activation(out=rsd[:, hh, :], in_=rsd[:, hh, :], func=AF.Exp, scale=-0.5))
                    op('dve', reads=[ok, 'rsd.%d' % hh, 'gp'], writes=[ok])(
                        DVE.scalar_tensor_tensor(out=ot[:, sl, c, :], in0=ot[:, sl, c, :], scalar=gp[:, G_SUB:G_SUB + 1], in1=rsd[:, hh, :],
                                                 op0=ALU.mult, op1=ALU.mult))
                for m in range(8):
                    b = nb()
                    for c in range(8):
                        op('pe', reads=['ot%d.%d' % (sl, c), 'wout'], writes=[pb(b)])(
                            PE.matmul(ps[:, b, :], lhsT=wout[:, c, m * 128:(m + 1) * 128], rhs=ot[:, sl, c, :], start=(c == 0), stop=(c == 7)))
                    xk = 'x%d.%d' % (sl, m)
                    op('dve', reads=[pb(b), xk], writes=[xk])(
                        DVE.tensor_tensor(out=xT[:, sl, m, :], in0=ps[:, b, :], in1=xT[:, sl, m, :], op=ALU.add))
                    op('act', reads=[xk], writes=['sq.%d' % m])(ACT.activation(out=sq[:, m, :], in_=xT[:, sl, m, :], func=AF.Square))
                op('sp', reads=['x%d.%d' % (sl, c) for c in range(8)], dma='c1xs%d' % sl)(
                    SP.dma_start(out=xTb[:, t0:t0 + 512].rearrange("(c p) t -> p c t", p=128), in_=xT[:, sl]))
                b = nb()
                for c in range(8):
                    op('pe', reads=['sq.%d' % c, 'ones_b'], writes=[pb(b)])(
                        PE.matmul(ps[:, b, :], lhsT=ones_b, rhs=sq[:, c, :], start=(c == 0), stop=(c == 7)))
                op('act', reads=[pb(b), 'epsb'], writes=['rs'])(
                    ACT.activation(out=rs, in_=ps[:, b, :], func=AF.Ln, bias=epsb, scale=1.0 / D_MODEL))
                op('act', reads=['rs'], writes=['rs'])(ACT.activation(out=rs, in_=rs, func=AF.Exp, scale=-0.5))
                for c in range(8):
                    e, E = ('dve', DVE)
                    op(e, reads=['x%d.%d' % (sl, c), 'rs', 'gp'], writes=['h2%d' % sl])(
                        E.scalar_tensor_tensor(out=h2[:, sl, c, :], in0=xT[:, sl, c, :], scalar=gp[:, G_LN2 + c:G_LN2 + c + 1],
                                               in1=rs, op0=ALU.mult, op1=ALU.mult))
                op('sp', reads=['h2%d' % sl], dma='c1hs%d' % sl)(
                    SP.dma_start(out=h2d[:, 1 + t0:1 + t0 + 512].rearrange("(c p) t -> p c t", p=128), in_=h2[:, sl]))
        sc.barrier()

        with ExitStack() as _st:
            wg_h = _st.enter_context(nc.sbuf_tensor(un("c2_wg"), [128, 8, D_FF], BF16))
            wu_h = _st.enter_context(nc.sbuf_tensor(un("c2_wu"), [128, 8, D_FF], BF16))
            wd_h = _st.enter_context(nc.sbuf_tensor(un("c2_wd"), [128, NJ, D_MODEL], BF16))
            wg, wu, wd = wg_h.ap(), wu_h.ap(), wd_h.ap()
            with ExitStack() as _st:
                stg_h = _st.enter_context(nc.sbuf_tensor(un("c2_stage"), [128, 2, D_FF], F32))
                stg = stg_h.ap()
                for c in range(8):
                    load_w(stg, wg[:, c, :], wg_d[l, c * 128:(c + 1) * 128, :], D_FF, 'wg.%d' % c)
                    load_w(stg, wu[:, c, :], wu_d[l, c * 128:(c + 1) * 128, :], D_FF, 'wu.%d' % c)
                for j in range(NJ):
                    load_w(stg, wd[:, j, :], wd_d[l, j * 128:(j + 1) * 128, :], D_MODEL, 'wd.%d' % j)
                sc.barrier()
            with ExitStack() as _st:
                TW = 510
                h2 = _st.enter_context(nc.sbuf_tensor(un("c2_h2"), [128, 2, 8, TW + 2], BF16)).ap()
                x1 = _st.enter_context(nc.sbuf_tensor(un("c2_x1"), [128, 3, TW], F32)).ap()
                gb = _st.enter_context(nc.sbuf_tensor(un("c2_g"), [128, 2, TW], F32)).ap()
                act = _st.enter_context(nc.sbuf_tensor(un("c2_act"), [128, NJ, TW], BF16)).ap()
                tiles2 = [(t0, min(TW, S - t0)) for t0 in range(0, S, TW)]

                def load_c2(i):
                    sl = i % 2
                    t0, W = tiles2[i]
                    op('sp', writes=['h2%d' % sl], dma='c2h%d' % sl)(
                        SP.dma_start(out=h2[:, sl, :, 0:W + 2], in_=h2d[:, t0:t0 + W + 2].rearrange("(c p) t -> p c t", p=128)))

                xc = [0]

                def load_x1(i, m):
                    t0, W = tiles2[i]
                    xs_ = (i * 8 + m) % 3
                    op('sp', writes=['x1c%d' % xs_], dma='c2x%d' % xs_)(
                        SP.dma_start(out=x1[:, xs_, 0:W], in_=xTb[m * 128:(m + 1) * 128, t0:t0 + W]))

                load_c2(0)
                jc = [0]
                for i in range(len(tiles2)):
                    sl = i % 2
                    t0, W = tiles2[i]
                    if i + 1 < len(tiles2):
                        load_c2(i + 1)
                    hk_ = 'h2%d' % sl
                    for j in range(NJ):
                        js = jc[0] % 2
                        jc[0] += 1
                        bg = nb()
                        for c in range(8):
                            op('pe', reads=[hk_, 'wg'], writes=[pb(bg)])(
                                PE.matmul(ps[:, bg, 0:W + 2], lhsT=wg[:, c, j * 128:(j + 1) * 128], rhs=h2[:, sl, c, 0:W + 2], start=(c == 0), stop=(c == 7)))
                        bu = nb()
                        for c in range(8):
                            op('pe', reads=[hk_, 'wu'], writes=[pb(bu)])(
                                PE.matmul(ps[:, bu, 0:W], lhsT=wu[:, c, j * 128:(j + 1) * 128], rhs=h2[:, sl, c, 1:W + 1], start=(c == 0), stop=(c == 7)))
                        cw = G_CW + 3 * j
                        op('act', reads=[pb(bg), 'gp'], writes=['g%d' % js])(
                            ACT.activation(out=gb[:, js, 0:W], in_=ps[:, bg, 0:W], func=AF.Identity, bias=gp[:, G_CB + j:G_CB + j + 1],
                                           scale=gp[:, cw:cw + 1]))
                        op('dve', reads=[pb(bg), 'g%d' % js, 'gp'], writes=['g%d' % js])(
                            DVE.scalar_tensor_tensor(out=gb[:, js, 0:W], in0=ps[:, bg, 1:W + 1], scalar=gp[:, cw + 1:cw + 2], in1=gb[:, js, 0:W],
                                                     op0=ALU.mult, op1=ALU.add))
                        op('dve', reads=[pb(bg), 'g%d' % js, 'gp'], writes=['g%d' % js])(
                            DVE.scalar_tensor_tensor(out=gb[:, js, 0:W], in0=ps[:, bg, 2:W + 2], scalar=gp[:, cw + 2:cw + 3], in1=gb[:, js, 0:W],
                                                     op0=ALU.mult, op1=ALU.add))
                        op('act', reads=['g%d' % js], writes=['g%d' % js])(ACT.activation(out=gb[:, js, 0:W], in_=gb[:, js, 0:W], func=AF.Silu))
                        op('dve', reads=['g%d' % js, pb(bu)], writes=['act.%d' % j])(
                            DVE.tensor_tensor(out=act[:, j, 0:W], in0=ps[:, bu, 0:W], in1=gb[:, js, 0:W], op=ALU.mult))
                        if j >= NJ - 3:
                            load_x1(i, j - (NJ - 3))
                    for m in range(8):
                        b = nb()
                        for j in range(NJ):
                            op('pe', reads=['act.%d' % j, 'wd'], writes=[pb(b)])(
                                PE.matmul(ps[:, b, 0:W], lhsT=wd[:, j, m * 128:(m + 1) * 128], rhs=act[:, j, 0:W], start=(j == 0), stop=(j == NJ - 1)))
                        xs_ = (i * 8 + m) % 3
                        xk = 'x1c%d' % xs_
                        op('dve', reads=[pb(b), xk], writes=[xk])(
                            DVE.tensor_tensor(out=x1[:, xs_, 0:W], in0=ps[:, b, 0:W], in1=x1[:, xs_, 0:W], op=ALU.add))
                        op('sp', reads=[xk], dma='c2xs%d' % xs_)(
                            SP.dma_start(out=xTa[m * 128:(m + 1) * 128, t0:t0 + W], in_=x1[:, xs_, 0:W]))
                        if m + 3 < 8:
                            load_x1(i, m + 3)
        sc.barrier()

    with ExitStack() as _st:
        xt_h = _st.enter_context(nc.sbuf_tensor(un("z_xt"), [128, 2, 8, 512], F32))
        y_h = _st.enter_context(nc.sbuf_tensor(un("z_y"), [128, 2, 4, D_MODEL], F32))
        xt, yb = xt_h.ap(), y_h.ap()
        for i in range(NT):
            sl = i % 2
            op('sp', writes=['zx%d' % sl], dma='zl%d' % sl)(
                SP.dma_start(out=xt[:, sl], in_=xTa[:, i * 512:(i + 1) * 512].rearrange("(c p) t -> p c t", p=128)))
            for s in range(4):
                for half in range(2):
                    b = nb()
                    for cc in range(4):
                        c = half * 4 + cc
                        op('pe', reads=['zx%d' % sl, 'ident'], writes=[pb(b)])(
                            PE.transpose(ps[:, b, cc * 128:(cc + 1) * 128], xt[:, sl, c, s * 128:(s + 1) * 128], ident))
                    if half == 0:
                        op('dve', reads=[pb(b)], writes=['zy%d.%d.%d' % (sl, s, half)])(
                            DVE.tensor_copy(out=yb[:, sl, s, half * 512:(half + 1) * 512], in_=ps[:, b, :]))
                    else:
                        op('act', reads=[pb(b)], writes=['zy%d.%d.%d' % (sl, s, half)])(
                            ACT.copy(out=yb[:, sl, s, half * 512:(half + 1) * 512], in_=ps[:, b, :]))
            op('sp', reads=['zy%d.%d.%d' % (sl, s, hf) for s in range(4) for hf in range(2)], dma='zs%d' % sl)(
                SP.dma_start(out=y_out[i * 512:(i + 1) * 512, :].rearrange("(s p) f -> p s f", p=128), in_=yb[:, sl]))
    sc.barrier()
    return nc, sc


def _rope_tables(S):
    pos = np.arange(S, dtype=np.float32)
    fm = (np.float32(MLA_THETA) ** (-np.arange(16, dtype=np.float32) * np.float32(2.0) / np.float32(32))).astype(np.float32)
    angm = (pos[:, None] * fm[None, :]).astype(np.float32)
    tabm = np.zeros((96, 2, S), np.float32)
    tabm[:, 0, :] = 1.0
    tabm[64:80, 0, :] = np.cos(angm).T
    tabm[80:96, 0, :] = np.cos(angm).T
    tabm[64:80, 1, :] = np.sin(angm).T
    tabm[80:96, 1, :] = np.sin(angm).T
    fd = (np.float32(ROPE_THETA) ** (-np.arange(8, dtype=np.float32) * np.float32(2.0) / np.float32(16))).astype(np.float32)
    angd = (pos[:, None] * fd[None, :]).astype(np.float32)
    tabd = np.zeros((128, 2, S), np.float32)
    tabd[:, 0, :] = 1.0
    for m in range(2):
        o = 64 * m
        tabd[o:o + 8, 0, :] = np.cos(angd).T
        tabd[o + 8:o + 16, 0, :] = np.cos(angd).T
        tabd[o:o + 8, 1, :] = np.sin(angd).T
        tabd[o + 8:o + 16, 1, :] = np.sin(angd).T
    return tabm, tabd


def _rot_mats():
    r96 = np.zeros((96, 96), np.float32)
    for i in range(16):
        r96[64 + 16 + i, 64 + i] = -1.0
        r96[64 + i, 64 + 16 + i] = 1.0
    r128 = np.zeros((128, 128), np.float32)
    for m in range(2):
        o = 64 * m
        for i in range(8):
            r128[o + 8 + i, o + i] = -1.0
            r128[o + i, o + 8 + i] = 1.0
    e32 = np.zeros((32, 96), np.float32)
    for i in range(32):
        e32[i, 64 + i] = 1.0
    return r96, r128, e32


def _prep_shared(S, depth, p):
    f = lambda a: np.ascontiguousarray(np.asarray(a, dtype=np.float32))
    w_kv = f(p['w_kv_up'])[:depth].reshape(depth, 128, 8, 128)
    wkn = np.zeros((depth, 128, 8, 96), np.float32)
    wkn[:, :, :, 0:64] = w_kv[:, :, :, 0:64]
    wv = np.ascontiguousarray(w_kv[:, :, :, 64:128]).reshape(depth, 128, 512)
    gpack = np.zeros((depth, 128, NG), np.float32)
    for l in range(depth):
        gpack[l, :, G_LN1:G_LN1 + 8] = f(p['ln1_g'])[l].reshape(8, 128).T
        gpack[l, :, G_QN:G_QN + 2] = f(p['mla_q_norm_g'])[l].reshape(2, 128).T
        gpack[l, :, G_KVN] = f(p['mla_kv_norm_g'])[l]
        gpack[l, 0:96, G_MQ] = f(p['mla_qn_g'])[l]
        gpack[l, 0:96, G_MK] = f(p['mla_kn_g'])[l]
        gpack[l, 0:64, G_DQ] = f(p['diff_qn_g'])[l]
        gpack[l, 64:128, G_DQ] = f(p['diff_qn_g'])[l]
        gpack[l, 0:64, G_DK] = f(p['diff_kn_g'])[l]
        gpack[l, 64:128, G_DK] = f(p['diff_kn_g'])[l]
        gpack[l, :, G_SUB] = f(p['diff_subln_g'])[l]
        gpack[l, :, G_LN2:G_LN2 + 8] = f(p['ln2_g'])[l].reshape(8, 128).T
        cw = f(p['conv_w'])[l]
        for j in range(NJ):
            for k in range(3):
                gpack[l, :, G_CW + 3 * j + k] = cw[k, j * 128:(j + 1) * 128]
        gpack[l, :, G_CB:G_CB + NJ] = f(p['conv_b'])[l].reshape(NJ, 128).T
    lamv = np.zeros((depth, 128, 4, 64), np.float32)
    for l in range(depth):
        for a, nm in enumerate(('lambda_q1', 'lambda_k1', 'lambda_q2', 'lambda_k2')):
            lamv[l, :, a, :] = f(p[nm])[l][None, :]
    tabm, tabd = _rope_tables(S)
    r96, r128, e32 = _rot_mats()
    return {
        "w_in": f(p['w_in'])[:depth], "wq": f(p['w_q_up'])[:depth], "wkn": wkn.reshape(depth, 128, 768), "wv": wv,
        "wout": f(p['w_out'])[:depth], "wg": f(p['w_gate'])[:depth], "wu": f(p['w_up'])[:depth], "wd": f(p['w_down'])[:depth],
        "gpack": gpack, "lamv": lamv.reshape(depth, 128, 256), "ident": np.eye(128, dtype=np.float32),
        "r96": r96, "r128": r128, "e32": e32, "tabm": tabm, "tabd": tabd,
    }


def run_trunk(seqs, params, depth=DEPTH):
    S = seqs[0].shape[0]
    lam_inits = [0.8 - 0.6 * math.exp(-0.3 * l) for l in range(depth)]
    nc, sc = build(S, depth, lam_inits)
    shared = _prep_shared(S, depth, params)
    n = len(seqs)
    in_maps = []
    for c in range(8):
        m = dict(shared)
        m["x"] = np.ascontiguousarray(seqs[c % n], dtype=np.float32)
        in_maps.append(m)
    res = run_bass_kernel_spmd(nc, in_maps, core_ids=list(range(8)))
    return [np.asarray(res.results[c]["y"], dtype=np.float32) for c in range(n)]


def kernel(x_prompt, x_sample, ln1_g, w_in, mla_q_norm_g, w_q_up, mla_kv_norm_g, w_kv_up,
           mla_qn_g, mla_kn_g, diff_qn_g, diff_kn_g, lambda_q1, lambda_k1, lambda_q2,
           lambda_k2, diff_subln_g, w_out, ln2_g, w_gate, conv_w, conv_b, w_up, w_down):
    params = dict(ln1_g=ln1_g, w_in=w_in, mla_q_norm_g=mla_q_norm_g, w_q_up=w_q_up,
                  mla_kv_norm_g=mla_kv_norm_g, w_kv_up=w_kv_up, mla_qn_g=mla_qn_g,
                  mla_kn_g=mla_kn_g, diff_qn_g=diff_qn_g, diff_kn_g=diff_kn_g,
                  lambda_q1=lambda_q1, lambda_k1=lambda_k1, lambda_q2=lambda_q2,
                  lambda_k2=lambda_k2, diff_subln_g=diff_subln_g, w_out=w_out, ln2_g=ln2_g,
                  w_gate=w_gate, conv_w=conv_w, conv_b=conv_b, w_up=w_up, w_down=w_down)
    xp = np.asarray(x_prompt, dtype=np.float32)
    xs = np.asarray(x_sample, dtype=np.float32)
    seqs = [xp[b] for b in range(xp.shape[0])] + [xs[b] for b in range(xs.shape[0])]
    outs = run_trunk(seqs, params, DEPTH)
    y_prompt = np.stack(outs[:xp.shape[0]], axis=0)
    y_sample = np.stack(outs[xp.shape[0]:], axis=0)
    return (y_prompt, y_sample)
```
